# Optimizing a Trainium2 kernel written in Bass

```python
import jax, jax.numpy as jnp
from jax import lax
import numpy as np

D_MODEL = 1024
BATCH = 1
SEQ = 16384
DEPTH = 4

GLA_HEADS = 4
GLA_DK_HEAD = 64
GLA_DV_HEAD = 128
GLA_DK = GLA_HEADS * GLA_DK_HEAD
GLA_DV = GLA_HEADS * GLA_DV_HEAD
GLA_GATE_RANK = 16
GLA_GATE_TAU = 16.0
GLA_CHUNK = 64
SWA_Q_HEADS = 8
SWA_KV_HEADS = 2
SWA_HEAD_DIM = 64
SWA_WINDOW = 128
SWA_BLOCK = 128
SWA_Q_DIM = SWA_Q_HEADS * SWA_HEAD_DIM
SWA_KV_DIM = SWA_KV_HEADS * SWA_HEAD_DIM
POOL_GROUPS = 4
POOL_GROUP_DIM = 128
POOL_DIM = POOL_GROUPS * POOL_GROUP_DIM
POOL_WINDOWS = (2, 4, 8, 16)
N_BRANCHES = 3
D_FF = 2816
CONV_WIDTH = 3
RMS_EPS = 1e-6

SPLIT_SIZES = (GLA_DK, GLA_DK, GLA_DV, GLA_GATE_RANK, GLA_DV,
               SWA_Q_DIM, SWA_KV_DIM, SWA_KV_DIM,
               POOL_DIM,
               N_BRANCHES * D_MODEL)
N_IN = sum(SPLIT_SIZES)
SPLIT_POINTS = tuple(int(v) for v in np.cumsum(SPLIT_SIZES)[:-1])

kernel_name = "hybrid_gla_swa_pool_gated_convffn"


def rmsnorm(x, gain):
    xf = x.astype(jnp.float32)
    var = jnp.mean(xf * xf, axis=-1, keepdims=True)
    return (xf * lax.rsqrt(var + RMS_EPS) * gain.astype(jnp.float32)).astype(x.dtype)


def gla_mixer(q, k, v, gate_low, r, w_gate_up, b_gate, norm_gain):
    f32 = jnp.float32
    B, T, _ = q.shape
    nc = T // GLA_CHUNK
    logit = gate_low.astype(f32) @ w_gate_up.astype(f32) + b_gate.astype(f32)
    log_alpha = jax.nn.log_sigmoid(logit) / GLA_GATE_TAU

    def to_chunks(t, d):
        return t.astype(f32).reshape(B, nc, GLA_CHUNK, GLA_HEADS, d).transpose(1, 0, 3, 2, 4)

    qc = to_chunks(q, GLA_DK_HEAD) * (GLA_DK_HEAD ** -0.5)
    kc = to_chunks(k, GLA_DK_HEAD)
    vc = to_chunks(v, GLA_DV_HEAD)
    gc = to_chunks(log_alpha, GLA_DK_HEAD)
    causal = jnp.tril(jnp.ones((GLA_CHUNK, GLA_CHUNK), dtype=bool))

    def step(S, inp):
        qi, ki, vi, gi = inp
        b = jnp.cumsum(gi, axis=2)
        diff = b[:, :, :, None, :] - b[:, :, None, :, :]
        decay = jnp.exp(jnp.where(causal[None, None, :, :, None], diff, -jnp.inf))
        attn = jnp.einsum('bhid,bhjd,bhijd->bhij', qi, ki, decay)
        o = jnp.einsum('bhij,bhjv->bhiv', attn, vi) \
            + jnp.einsum('bhid,bhdv->bhiv', qi * jnp.exp(b), S)
        b_last = b[:, :, -1:, :]
        S_new = jnp.exp(b_last[:, :, 0, :])[..., None] * S \
            + jnp.einsum('bhjd,bhjv->bhdv', ki * jnp.exp(b_last - b), vi)
        return S_new, o

    S0 = jnp.zeros((B, GLA_HEADS, GLA_DK_HEAD, GLA_DV_HEAD), f32)
    _, o = lax.scan(step, S0, (qc, kc, vc, gc))
    o = o.transpose(1, 0, 3, 2, 4).reshape(B, T, GLA_HEADS, GLA_DV_HEAD)
    var = jnp.mean(o * o, axis=-1, keepdims=True)
    o = o * lax.rsqrt(var + RMS_EPS) * norm_gain.astype(f32).reshape(GLA_HEADS, GLA_DV_HEAD)
    o = o.reshape(B, T, GLA_DV) * jax.nn.silu(r.astype(f32))
    return o.astype(q.dtype)


def swa_mixer(q, k, v, sinks):
    f32 = jnp.float32
    B, T, _ = q.shape
    nb = T // SWA_BLOCK
    G = SWA_Q_HEADS // SWA_KV_HEADS
    qb = q.astype(f32).reshape(B, nb, SWA_BLOCK, SWA_KV_HEADS, G, SWA_HEAD_DIM) * (SWA_HEAD_DIM ** -0.5)
    kb = k.astype(f32).reshape(B, nb, SWA_BLOCK, SWA_KV_HEADS, SWA_HEAD_DIM)
    vb = v.astype(f32).reshape(B, nb, SWA_BLOCK, SWA_KV_HEADS, SWA_HEAD_DIM)

    def with_prev(t):
        prev = jnp.concatenate([jnp.zeros_like(t[:, :1]), t[:, :-1]], axis=1)
        return jnp.concatenate([prev, t], axis=2)

    kw, vw = with_prev(kb), with_prev(vb)
    s = jnp.einsum('bnqhgd,bnkhd->bnhgqk', qb, kw)
    blk = jnp.arange(nb)[:, None, None] * SWA_BLOCK
    q_pos = blk + jnp.arange(SWA_BLOCK)[None, :, None]
    k_pos = blk - SWA_BLOCK + jnp.arange(2 * SWA_BLOCK)[None, None, :]
    valid = (k_pos <= q_pos) & (q_pos - k_pos < SWA_WINDOW) & (k_pos >= 0)
    s = jnp.where(valid[None, :, None, None], s, -jnp.inf)
    sink = sinks.astype(f32).reshape(1, 1, SWA_KV_HEADS, G, 1, 1)
    m = jnp.maximum(jnp.max(s, axis=-1, keepdims=True), sink)
    p = jnp.exp(s - m)
    denom = jnp.sum(p, axis=-1, keepdims=True) + jnp.exp(sink - m)
    o = jnp.einsum('bnhgqk,bnkhd->bnqhgd', p / denom, vw)
    return o.reshape(B, T, SWA_Q_DIM).astype(q.dtype)


def pool_mixer(u, w_group, scale):
    f32 = jnp.float32
    B, T, _ = u.shape
    uf = u.astype(f32)
    cs = jnp.cumsum(uf, axis=1)
    pos = jnp.arange(1, T + 1, dtype=f32)
    means = []
    for g, w in enumerate(POOL_WINDOWS):
        cg = cs[..., g * POOL_GROUP_DIM:(g + 1) * POOL_GROUP_DIM]
        shifted = jnp.pad(cg, ((0, 0), (w, 0), (0, 0)))[:, :T]
        cnt = jnp.minimum(pos, float(w))
        means.append((cg - shifted) / cnt[None, :, None])
    pooled = jnp.stack(means, axis=2)
    d = pooled - uf.reshape(B, T, POOL_GROUPS, POOL_GROUP_DIM)
    y = jnp.einsum('btgc,gcd->btgd', d, w_group.astype(f32)).reshape(B, T, POOL_DIM)
    return (y * scale.astype(f32)).astype(u.dtype)


def conv_ffn(h, w_up, conv_w, conv_b, w_down):
    T = h.shape[1]
    up = h @ w_up
    padded = jnp.pad(up, ((0, 0), (CONV_WIDTH - 1, 0), (0, 0)))
    conv = conv_b + conv_w[0] * padded[:, 0:T]
    for i in range(1, CONV_WIDTH):
        conv = conv + conv_w[i] * padded[:, i:i + T]
    gate, val = jnp.split(conv, 2, axis=-1)
    return (jax.nn.gelu(gate, approximate=True) * val) @ w_down


def setup_inputs(seed: int = 0) -> dict:
    key = jax.random.key(seed)
    ks = jax.random.split(key, 20)
    L = DEPTH

    def nrm(k, shape, scale):
        return jax.random.normal(k, shape, jnp.float32) * scale

    return {
        "x": nrm(ks[0], (BATCH, SEQ, D_MODEL), 1.0),
        "norm_mix_pre": 1.0 + nrm(ks[1], (L, D_MODEL), 0.05),
        "norm_mix_post": 1.0 + nrm(ks[2], (L, D_MODEL), 0.05),
        "norm_ffn_pre": 1.0 + nrm(ks[3], (L, D_MODEL), 0.05),
        "norm_ffn_post": 1.0 + nrm(ks[4], (L, D_MODEL), 0.05),
        "w_in": nrm(ks[5], (L, D_MODEL, N_IN), D_MODEL ** -0.5),
        "gla_w_gate_up": nrm(ks[6], (L, GLA_GATE_RANK, GLA_DK), GLA_GATE_RANK ** -0.5),
        "gla_b_gate": nrm(ks[7], (L, GLA_DK), 0.1),
        "gla_norm": 1.0 + nrm(ks[8], (L, GLA_DV), 0.05),
        "swa_sinks": nrm(ks[9], (L, SWA_Q_HEADS), 0.5),
        "pool_w": nrm(ks[10], (L, POOL_GROUPS, POOL_GROUP_DIM, POOL_GROUP_DIM), POOL_GROUP_DIM ** -0.5),
        "pool_scale": 1.0 + nrm(ks[11], (L, POOL_DIM), 0.05),
        "w_branch_gla": nrm(ks[12], (L, GLA_DV, D_MODEL), GLA_DV ** -0.5),
        "w_branch_swa": nrm(ks[13], (L, SWA_Q_DIM, D_MODEL), SWA_Q_DIM ** -0.5),
        "w_branch_pool": nrm(ks[14], (L, POOL_DIM, D_MODEL), POOL_DIM ** -0.5),
        "w_out": nrm(ks[15], (L, D_MODEL, D_MODEL), D_MODEL ** -0.5),
        "ffn_w_up": nrm(ks[16], (L, D_MODEL, 2 * D_FF), D_MODEL ** -0.5),
        "ffn_conv_w": nrm(ks[17], (L, CONV_WIDTH, 2 * D_FF), CONV_WIDTH ** -0.5),
        "ffn_conv_b": nrm(ks[18], (L, 2 * D_FF), 0.01),
        "ffn_w_down": nrm(ks[19], (L, D_FF, D_MODEL), D_FF ** -0.5),
    }


def reference(x, norm_mix_pre, norm_mix_post, norm_ffn_pre, norm_ffn_post, w_in,
              gla_w_gate_up, gla_b_gate, gla_norm, swa_sinks, pool_w, pool_scale,
              w_branch_gla, w_branch_swa, w_branch_pool, w_out,
              ffn_w_up, ffn_conv_w, ffn_conv_b, ffn_w_down):
    B, T, _ = x.shape
    for l in range(DEPTH):
        h = rmsnorm(x, norm_mix_pre[l])
        proj = h @ w_in[l]
        (g_q, g_k, g_v, g_low, g_r, s_q, s_k, s_v, p_u, gates) = jnp.split(proj, SPLIT_POINTS, axis=-1)
        y_a = gla_mixer(g_q, g_k, g_v, g_low, g_r, gla_w_gate_up[l], gla_b_gate[l], gla_norm[l]) @ w_branch_gla[l]
        y_b = swa_mixer(s_q, s_k, s_v, swa_sinks[l]) @ w_branch_swa[l]
        y_c = pool_mixer(p_u, pool_w[l], pool_scale[l]) @ w_branch_pool[l]
        gate = jax.nn.sigmoid(gates.reshape(B, T, N_BRANCHES, D_MODEL))
        merged = gate[:, :, 0] * y_a + gate[:, :, 1] * y_b + gate[:, :, 2] * y_c
        x = x + rmsnorm(merged @ w_out[l], norm_mix_post[l])
        h = rmsnorm(x, norm_ffn_pre[l])
        f = conv_ffn(h, ffn_w_up[l], ffn_conv_w[l], ffn_conv_b[l], ffn_w_down[l])
        x = x + rmsnorm(f, norm_ffn_post[l])
    return x
```

```python
import os
import numpy as np
import concourse.bass as bass
import concourse.mybir as mybir
from concourse.bass_utils import run_bass_kernel_spmd

F32 = mybir.dt.float32
BF16 = mybir.dt.bfloat16
AF = mybir.ActivationFunctionType
ALU = mybir.AluOpType

NCORES = 8
T_ALL = 16384
TC = 2048
D = 1024
DEPTH = 4
N = 512
NT = TC // N
NB = TC // 128
DFF = 2816
NFC = DFF // 128
EPS = 1e-6
O_GQ, O_GK, O_GV, O_GLOW, O_GR, O_SQ, O_SK, O_SV, O_PU, O_GATE = 0, 256, 512, 1024, 1040, 1552, 2064, 2192, 2320, 2832
CP = 220
XA = 584
XB = 16
SLOT = 3072
NSLOT = 4


class KB:
    def __init__(self, nc):
        self.nc = nc
        self.engs = {"pe": nc.tensor, "act": nc.scalar, "dve": nc.vector, "pool": nc.gpsimd, "sp": nc.sync}
        self.esem, self.ecnt, self._ctx = {}, {}, []
        for n in ("pe", "act", "dve", "pool"):
            cm = nc.semaphore("es_" + n)
            self.esem[n] = cm.__enter__()
            self._ctx.append(cm)
            self.ecnt[n] = 0
        self.known = {n: {} for n in self.engs}
        self.lastw, self.readers = {}, {}
        self.dsem, self.dcnt = {}, {}
        self.nwaits = 0
        self.nins = 0

    def dma_sem(self, key):
        if key not in self.dsem:
            cm = self.nc.semaphore("ds_" + str(key))
            self.dsem[key] = cm.__enter__()
            self._ctx.append(cm)
            self.dcnt[key] = 0
        return self.dsem[key]

    def close(self):
        for cm in reversed(self._ctx):
            cm.__exit__(None, None, None)

    def _wait(self, en, sem, val):
        k = self.known[en]
        key = sem.name
        if k.get(key, 0) >= val:
            return
        k[key] = val
        self.engs[en].wait_ge(sem, val)
        self.nwaits += 1

    def _deps(self, en, reads, writes):
        need = []
        for b in reads:
            w = self.lastw.get(b)
            if w is not None and not (en == "pe" and w[2] == "pe"):
                need.append(w)
        for b in writes:
            w = self.lastw.get(b)
            if w is not None and not (en == "pe" and w[2] == "pe"):
                need.append(w)
            for r in self.readers.get(b, ()):
                if not (en == "pe" and r[2] == "pe"):
                    need.append(r)
        for ev in need:
            self._wait(en, ev[0], ev[1])

    def _record(self, ev, reads, writes):
        for b in writes:
            self.lastw[b] = ev
            self.readers[b] = []
        for b in reads:
            lst = self.readers.setdefault(b, [])
            for i, r in enumerate(lst):
                if r[0] is ev[0]:
                    lst[i] = ev
                    break
            else:
                lst.append(ev)

    def op(self, en, fn, reads=(), writes=()):
        reads = [getattr(r, "nm", r) for r in reads]
        writes = [getattr(r, "nm", r) for r in writes]
        self._deps(en, reads, writes)
        ins = fn(self.engs[en])
        self.ecnt[en] += 1
        self.nins += 1
        ins.then_inc(self.esem[en], 1)
        self._record((self.esem[en], self.ecnt[en], en, False), reads, writes)
        return ins

    def dma(self, q, key, out, in_, reads=(), writes=()):
        reads = [getattr(r, "nm", r) for r in reads]
        writes = [getattr(r, "nm", r) for r in writes]
        sem = self.dma_sem(key)
        self._deps(q, reads, writes)
        ins = self.engs[q].dma_start(out=out, in_=in_)
        self.dcnt[key] += 16
        self.nins += 1
        ins.then_inc(sem, 16)
        self._record((sem, self.dcnt[key], "dma", True), reads, writes)
        return ins

    def wait_all(self, en, bufs):
        for b in bufs:
            w = self.lastw.get(b)
            if w is not None:
                self._wait(en, w[0], w[1])

    def barrier_dma(self, en):
        for key, sem in self.dsem.items():
            if self.dcnt[key] > 0:
                self._wait(en, sem, self.dcnt[key])

    def barrier(self):
        for en in ("pe", "act", "dve", "pool", "sp"):
            for n2 in ("pe", "act", "dve", "pool"):
                if n2 != en and self.ecnt[n2] > 0:
                    self._wait(en, self.esem[n2], self.ecnt[n2])
            for key, sem in self.dsem.items():
                if self.dcnt[key] > 0:
                    self._wait(en, sem, self.dcnt[key])


class StopBuild(Exception):
    pass


class Rot:
    def __init__(self, name, aps):
        self.name, self.aps, self.i = name, aps, 0
        self.gen = [0] * len(aps)

    def get(self):
        i = self.i % len(self.aps)
        self.i += 1
        self.gen[i] += 1
        return RB(self, i, self.gen[i])


class RB:
    def __init__(self, rot, i, gen):
        self.rot, self.i, self.gen = rot, i, gen
        self.ap = rot.aps[i]
        self.n = "%s%d" % (rot.name, i)

    @property
    def nm(self):
        assert self.rot.gen[self.i] == self.gen, "stale rotating buffer %s" % self.n
        return self.n


def _wspec():
    s = {}
    s["gkl"] = (8, 272)
    s["gq"] = (8, 256)
    s["gv0"] = (8, 256); s["gv1"] = (8, 256)
    s["gr0"] = (8, 256); s["gr1"] = (8, 256)
    s["sq0"] = (8, 256); s["sq1"] = (8, 256)
    s["skv"] = (8, 256)
    s["pu0"] = (8, 256); s["pu1"] = (8, 256)
    s["pw"] = (4, 128)
    for m in range(8):
        s["gt%d" % m] = (8, 384)
        s["br%d" % m] = (12, 128)
        s["wo%d" % m] = (8, 128)
        s["dn%d" % m] = (NFC, 128)
    for j in range(NFC):
        s["up%d" % j] = (8, 256)
    off, o = {}, 0
    for k, (nk, wc) in s.items():
        off[k] = o
        o += 128 * nk * wc
    return s, off, o


WSPEC, WOFF, WTOT = _wspec()


def _img(w):
    K, wc = w.shape
    nk = K // 128
    return np.ascontiguousarray(w.reshape(nk, 128, wc).transpose(1, 0, 2)).reshape(128, nk * wc)


def _pack_weights(inp, l):
    w_in = inp["w_in"][l]
    out = np.empty((WTOT,), np.float32)

    def put(name, mat):
        nk, wc = WSPEC[name]
        assert mat.shape == (nk * 128, wc), (name, mat.shape)
        out[WOFF[name]:WOFF[name] + 128 * nk * wc] = _img(mat).reshape(-1)

    put("gkl", np.concatenate([w_in[:, O_GK:O_GK + 256], w_in[:, O_GLOW:O_GLOW + 16]], axis=1))
    put("gq", w_in[:, O_GQ:O_GQ + 256])
    put("gv0", w_in[:, O_GV:O_GV + 256]); put("gv1", w_in[:, O_GV + 256:O_GV + 512])
    put("gr0", w_in[:, O_GR:O_GR + 256]); put("gr1", w_in[:, O_GR + 256:O_GR + 512])
    sqc = []
    for j in range(4):
        for g in range(2):
            h = g * 4 + j
            sqc.append(w_in[:, O_SQ + h * 64:O_SQ + (h + 1) * 64])
    sqc = np.concatenate(sqc, axis=1)
    put("sq0", sqc[:, :256]); put("sq1", sqc[:, 256:])
    put("skv", w_in[:, O_SK:O_SK + 256])
    put("pu0", w_in[:, O_PU:O_PU + 256]); put("pu1", w_in[:, O_PU + 256:O_PU + 512])
    pw = inp["pool_w"][l]
    put("pw", pw.reshape(4 * 128, 128))
    wg_, ws_, wp_ = inp["w_branch_gla"][l], inp["w_branch_swa"][l], inp["w_branch_pool"][l]
    perm = np.array([g * 256 + j * 64 + d for j in range(4) for g in range(2) for d in range(64)])
    ws_p = ws_[perm]
    wo_ = inp["w_out"][l]
    wup, wdn = inp["ffn_w_up"][l], inp["ffn_w_down"][l]
    for m in range(8):
        cs = slice(m * 128, (m + 1) * 128)
        put("gt%d" % m, np.concatenate([w_in[:, O_GATE + b * 1024 + m * 128:O_GATE + b * 1024 + (m + 1) * 128] for b in range(3)], axis=1))
        put("br%d" % m, np.concatenate([wg_[:, cs], ws_p[:, cs], wp_[:, cs]], axis=0))
        put("wo%d" % m, wo_[:, cs])
        put("dn%d" % m, wdn[:, cs])
    for j in range(NFC):
        put("up%d" % j, np.concatenate([wup[:, j * 128:(j + 1) * 128], wup[:, DFF + j * 128:DFF + (j + 1) * 128]], axis=1))
    return out


def _pack_small(inp):
    cp = np.zeros((128, DEPTH * CP), np.float32)
    for l in range(DEPTH):
        b = l * CP
        for i, k in enumerate(["norm_mix_pre", "norm_mix_post", "norm_ffn_pre", "norm_ffn_post"]):
            cp[:, b + i * 8:b + (i + 1) * 8] = inp[k][l].reshape(8, 128).T
        cp[:, b + 32:b + 36] = inp["gla_norm"][l].reshape(4, 128).T
        cp[:, b + 36:b + 40] = inp["pool_scale"][l].reshape(4, 128).T
        cw = inp["ffn_conv_w"][l]
        for i in range(3):
            cp[:, b + 40 + i * 44:b + 40 + (i + 1) * 44] = cw[i].reshape(44, 128).T
        cp[:, b + 172:b + 216] = inp["ffn_conv_b"][l].reshape(44, 128).T
        sk = inp["swa_sinks"][l]
        cp[0:64, b + 216:b + 220] = sk[0:4][None, :]
        cp[64:128, b + 216:b + 220] = sk[4:8][None, :]
    return cp


def _consts(core):
    j = np.arange(128)[:, None]
    i = np.arange(128)[None, :]
    cst = np.zeros((128, 384), np.float32)
    cst[:, 0:128] = np.where(j <= i, -1.0 / 16.0, 0.0)
    cst[:, 128:256] = np.where(j > i, -1.0 / 16.0, 0.0)
    cst[:, 256:384] = np.where(j <= i, 1.0, 0.0)
    pc = np.zeros((128, 96), np.float32)
    for c2 in range(8):
        pc[:, c2] = 1.0 if c2 < core else 0.0
        pc[:, 8 + c2] = 1.0 if c2 == core - 1 else 0.0
    pc[:, 16] = 1.0 if core > 0 else 0.0
    for g in range(4):
        w = 2 ** (g + 1)
        for t in range(16):
            pos = core * TC + t + 1
            pc[:, 32 + g * 16 + t] = 1.0 / min(pos, w)
    return cst, pc


def build(depth=DEPTH, stop_after=99, phase=None):
    nc = bass.Bass("TRN2", target_bir_lowering=False)
    xT_d = nc.dram_tensor("xT", [128, 8, TC], F32, kind="ExternalInput").ap()
    SMALLW = bool(os.environ.get("KDBG_SMALLW"))
    wp_d = nc.dram_tensor("wpack", [depth + 1, 128 * SLOT if SMALLW else WTOT], F32, kind="ExternalInput").ap()
    cp_d = nc.dram_tensor("cpack", [128, DEPTH * CP], F32, kind="ExternalInput").ap()
    wgb_d = nc.dram_tensor("wgb", [17, DEPTH * 256], F32, kind="ExternalInput").ap()
    cst_d = nc.dram_tensor("cst", [128, 384], F32, kind="ExternalInput").ap()
    pc_d = nc.dram_tensor("pcst", [128, 96], F32, kind="ExternalInput").ap()
    yT_d = nc.dram_tensor("yT", [128, 8, TC], F32, kind="ExternalOutput").ap() if phase != "A" else None
    if phase == "A":
        ccA_in = [nc.dram_tensor("payA", [128, XA], F32, kind="ExternalOutput")]
    else:
        ccA_in = [nc.dram_tensor("ccAi%d" % l, [128, XA], F32) for l in range(depth)]
    if phase == "B":
        ccA_out = [nc.dram_tensor("gA", [NCORES * 128, XA], F32, kind="ExternalInput")]
    else:
        ccA_out = [nc.dram_tensor("ccAo%d" % l, [NCORES * 128, XA], F32) for l in range(depth)]
    ccB_in = [nc.dram_tensor("ccBi%d" % l, [128, XB], F32) for l in range(depth)]
    if phase == "C":
        ccB_out = [nc.dram_tensor("gB", [NCORES * 128, XB], F32, kind="ExternalInput")]
    else:
        ccB_out = [nc.dram_tensor("ccBo%d" % l, [NCORES * 128, XB], F32) for l in range(depth)]
    dbg_outs = {}

    kb = KB(nc)
    ctxs = []

    def sb(name, shape, dt):
        cm = nc.sbuf_tensor(name, shape, dt)
        t = cm.__enter__()
        ctxs.append(cm)
        return t

    xT = sb("xTs", [128, 8, TC], F32)
    wsl = sb("wsl", [128, NSLOT, SLOT], BF16)
    cpk = sb("cpk", [128, DEPTH * CP], F32)
    wgb = sb("wgbs", [17, DEPTH * 256], F32)
    cst = sb("csts", [128, 384], F32)
    pcs = sb("pcs", [128, 96], F32)
    maskc = sb("maskc", [128, 4, 128], BF16)
    maskp = sb("maskp", [128, 4, 128], BF16)
    maskp0 = sb("maskp0", [128, 4, 128], BF16)
    ones_bf = sb("ones_bf", [128, 128], BF16)
    onespad = sb("onespad", [128, 2, 128], BF16)
    esink = sb("esink", [128, DEPTH * 4], F32)
    hT = sb("hT", [128, 8, N], BF16)
    zbuf = sb("zbuf", [128, 8, N], F32)
    S_run = sb("S_run", [128, 2, 128], F32)
    S_in = sb("S_in", [128, 2, 128], F32)
    Ploc = sb("Ploc", [128, NB + 1, 2], F32)
    Ebt = sb("Ebt", [128, 2, 2], F32)
    Am = sb("Am", [128, 2], F32)
    At8 = sb("At8", [128, 8], F32)
    halo = sb("halo", [128, 320], F32)
    uh = sb("uh", [128, 4, 16], F32)
    uph = sb("uph", [128, 44, 2], F32)
    xh = sb("xh", [128, 16], F32)
    glTt = sb("glTt", [32, N], F32)
    xt2 = sb("xt2", [128, 8, 2], F32)
    RB_EL = 28304
    Rb = sb("Rb", [128, RB_EL], BF16)
    o = [0]

    def carve(n):
        a = o[0]
        o[0] += n
        return a

    o[0] = 0
    a_Sloc = carve(NB * 256); a_eb = carve(1024); a_enb = carve(1024); a_Vg = carve(2048)
    a_kg = carve(1024); a_sr = carve(2048); a_glao = carve(2048)
    a_qs = carve(2048); a_swao = carve(2048); a_pc = carve(2048); a_mrg = carve(4096)
    a_qg = carve(2048); a_ks = carve(1280); a_vp = carve(1280); a_h2h = carve(16)
    assert o[0] <= RB_EL, o[0]
    assert a_qg >= NFC * N
    Sloc = Rb[:, a_Sloc:a_Sloc + NB * 256].rearrange("p (b q v) -> p b q v", b=NB, q=2)
    ebT = Rb[:, a_eb:a_eb + 1024].rearrange("p (q t) -> p q t", q=2)
    enbT = Rb[:, a_enb:a_enb + 1024].rearrange("p (q t) -> p q t", q=2)
    Vg = Rb[:, a_Vg:a_Vg + 2048].rearrange("p (b v) -> p b v", b=4)
    kgT = Rb[:, a_kg:a_kg + 1024].rearrange("p (q t) -> p q t", q=2)
    qgp = Rb[:, a_qg:a_qg + 2048].rearrange("p (h t) -> p h t", h=4)
    srT = Rb[:, a_sr:a_sr + 2048].rearrange("p (h t) -> p h t", h=4)
    glao = Rb[:, a_glao:a_glao + 2048].rearrange("p (h t) -> p h t", h=4)
    qsT = Rb[:, a_qs:a_qs + 2048].rearrange("p (j t) -> p j t", j=4)
    swao = Rb[:, a_swao:a_swao + 2048].rearrange("p (j t) -> p j t", j=4)
    pcT = Rb[:, a_pc:a_pc + 2048].rearrange("p (g t) -> p g t", g=4)
    mrg = Rb[:, a_mrg:a_mrg + 4096].rearrange("p (m t) -> p m t", m=8)
    ksp = Rb[:, a_ks:a_ks + 1280].rearrange("p (g t) -> p g t", g=2)
    Vpad = Rb[:, a_vp:a_vp + 1280].rearrange("p (b g d) -> p b g d", b=5, g=2)
    h2h = Rb[:, a_h2h:a_h2h + 16].rearrange("p (k t) -> p k t", k=8)
    assert a_ks >= NFC * N
    gT = Rb[:, 0:NFC * N].rearrange("p (f t) -> p f t", f=NFC)
    NSF, NSB = 6, 6
    sf_t = sb("scrf", [128, NSF, 528], F32)
    sbf_t = sb("scrb", [128, NSB, 512], BF16)
    SF = Rot("sf", [sf_t[:, i, :] for i in range(NSF)])
    SBF = Rot("sb", [sbf_t[:, i, :] for i in range(NSB)])
    cmps = nc.psum_tensor("ps", [128, 8, 512], F32)
    ps_t = cmps.__enter__()
    ctxs.append(cmps)
    PS = Rot("ps", [ps_t[:, i, :] for i in range(6)])
    PS_ST = ps_t[:, 6, :]
    PS_X = ps_t[:, 7, :]

    WS = Rot("w", [wsl[:, i, :] for i in range(NSLOT)])

    def W(l, name):
        nk, wc = WSPEC[name]
        rb = WS.get()
        n = nk * wc
        woff = 0 if SMALLW else WOFF[name]
        src = wp_d[l, woff:woff + 128 * n].rearrange("(p n) -> p n", p=128)
        kb.dma("pool", rb.n, rb.ap[:, 0:n], src, writes=[rb])
        return rb.ap[:, 0:n].rearrange("p (k c) -> p k c", k=nk), rb

    def mm(out, lhsT, rhs, start, stop, reads, wname):
        kb.op("pe", lambda e: e.matmul(out, lhsT=lhsT, rhs=rhs, start=start, stop=stop), reads=reads, writes=[wname])

    def act(out, in_, func, reads, writes, **kw):
        kb.op("act", lambda e: e.activation(out=out, in_=in_, func=func, **kw), reads=reads, writes=writes)

    def tt(out, in0, in1, op, reads, writes, en="dve"):
        kb.op(en, lambda e: e.tensor_tensor(out=out, in0=in0, in1=in1, op=op), reads=reads, writes=writes)

    def ts(out, in0, s1, s2, op0, op1, reads, writes, en="dve"):
        kb.op(en, lambda e: e.tensor_scalar(out=out, in0=in0, scalar1=s1, scalar2=s2, op0=op0, op1=op1), reads=reads, writes=writes)

    def stt(out, in0, scalar, in1, op0, op1, reads, writes, en="dve"):
        kb.op(en, lambda e: e.scalar_tensor_tensor(out=out, in0=in0, scalar=scalar, in1=in1, op0=op0, op1=op1), reads=reads, writes=writes)

    def cp(l, off, n=1):
        return cpk[:, l * CP + off:l * CP + off + n]

    def rstd_from(ps_ap, ps_name, ncols, inv_n):
        lnv = SF.get()
        act(lnv.ap[:, 0:ncols], ps_ap, AF.Ln, [ps_name], [lnv.nm], scale=inv_n, bias=EPS)
        r = SF.get()
        act(r.ap[:, 0:ncols], lnv.ap[:, 0:ncols], AF.Exp, [lnv.nm], [r.nm], scale=-0.5)
        return r

    def norm_tile(l, n, goff):
        cols = slice(n * N, (n + 1) * N)
        for kc in range(8):
            sq = SBF.get()
            act(sq.ap, xT[:, kc, cols], AF.Square, ["xT"], [sq.nm])
            mm(PS_ST, ones_bf[:, :], sq.ap, kc == 0, kc == 7, [sq.nm, "const"], "ps_st")
        r = rstd_from(PS_ST, "ps_st", N, 1.0 / D)
        for kc in range(8):
            stt(hT[:, kc, :], xT[:, kc, cols], cp(l, goff + kc), r.ap[:, 0:N], ALU.mult, ALU.mult, ["xT", r.nm, "const"], ["hT"])

    def post_norm_residual(l, n, goff):
        cols = slice(n * N, (n + 1) * N)
        r = rstd_from(PS_ST, "ps_st", N, 1.0 / D)
        for m in range(8):
            stt(zbuf[:, m, :], zbuf[:, m, :], cp(l, goff + m), r.ap[:, 0:N], ALU.mult, ALU.mult, ["zbuf", r.nm, "const"], ["zbuf"])
            tt(xT[:, m, cols], xT[:, m, cols], zbuf[:, m, :], ALU.add, ["xT", "zbuf"], ["xT"])

    def gla_L(l, glT, tb):
        pl = PS.get()
        mm(pl.ap[:, 0:256], glT[0:17, tb], wgb[0:17, l * 256:(l + 1) * 256], True, True, ["glT", "const"], pl.nm)
        e1 = SF.get()
        act(e1.ap[:, 0:256], pl.ap[:, 0:256], AF.Exp, [pl.nm], [e1.nm], scale=-1.0)
        Lt = SF.get()
        act(Lt.ap[:, 0:256], e1.ap[:, 0:256], AF.Ln, [e1.nm], [Lt.nm], bias=1.0)
        return Lt

    def gla_glow(l, Wgk, wn):
        pg = PS.get()
        for kc in range(8):
            mm(pg.ap[0:16, :], Wgk[:, kc, 256:272], hT[:, kc, :], kc == 0, kc == 7, [wn, "hT"], pg.nm)
        kb.op("dve", lambda e: e.memset(glTt[:], 1.0), reads=["glT"], writes=["glT"])
        act(glTt[0:16, :], pg.ap[0:16, :], AF.Identity, [pg.nm, "glT"], ["glT"])
        return glTt

    def v_tok(Wv0, wn0, Wv1, wn1, tb, out_ap, out_name):
        pv = PS.get()
        for half, (Wv, wn) in enumerate(((Wv0, wn0), (Wv1, wn1))):
            for kc in range(8):
                mm(pv.ap[:, half * 256:(half + 1) * 256], hT[:, kc, tb], Wv[:, kc, :], kc == 0, kc == 7, [wn, "hT"], pv.nm)
        act(out_ap, pv.ap, AF.Identity, [pv.nm], [out_name])

    kb.dma("sp", "ld", xT[:], xT_d, writes=["xT"])
    kb.dma("sp", "ld", cpk[:], cp_d, writes=["const"])
    kb.dma("sp", "ld", wgb[:], wgb_d, writes=["const"])
    kb.dma("sp", "ld", cst[:], cst_d, writes=["const"])
    kb.dma("sp", "ld", pcs[:], pc_d, writes=["const"])
    TriA, TriB, cmask = cst[:, 0:128], cst[:, 128:256], cst[:, 256:384]
    kb.op("dve", lambda e: e.memset(ones_bf[:], 1.0), writes=["const"])
    kb.op("dve", lambda e: e.memset(onespad[:], 0.0), writes=["const"])
    kb.op("dve", lambda e: e.memset(onespad[:, 0, 0:64], 1.0), reads=["const"], writes=["const"])
    kb.op("dve", lambda e: e.memset(onespad[:, 1, 64:128], 1.0), reads=["const"], writes=["const"])
    kb.op("dve", lambda e: e.memset(Rb[:], 0.0), writes=["Vpad", "ksT", "gT"])
    for j in range(4):
        kb.op("dve", lambda e, j=j: e.tensor_copy(out=maskc[:, j, :], in_=cmask), reads=["const"], writes=["const"])
        ts(maskp[:, j, :], cmask, -1.0, 1.0, ALU.mult, ALU.add, ["const"], ["const"])
    for j in range(4):
        ts(maskp0[:, j, :], maskp[:, j, :], pcs[:, 16:17], None, ALU.mult, ALU.bypass, ["const"], ["const"])
    for l in range(depth):
        act(esink[:, l * 4:(l + 1) * 4], cp(l, 216, 4), AF.Exp, ["const"], ["const"])

    kb.barrier()
    CUT = int(os.environ.get("KDBG_CUT", "0"))
    STOPL = int(os.environ.get("KDBG_STOPL", "0"))

    def cut(k):
        if CUT == k:
            raise StopBuild()

    def body():
      for l in range(depth):
          if stop_after <= 0 and l == STOPL:
              break
          if phase != "C":
              kb.op("dve", lambda e: e.memset(S_run[:], 0.0), reads=["S_run"], writes=["S_run"])
              kb.op("dve", lambda e: e.memset(Ploc[:, 0, :], 1.0), reads=["Ploc"], writes=["Ploc"])
              for n in range(NT):
                  norm_tile(l, n, 0)
                  Wgk, wn_gk = W(l, "gkl")
                  glT = gla_glow(l, Wgk, wn_gk)
                  Wv0, wn0 = W(l, "gv0")
                  Wv1, wn1 = W(l, "gv1")
                  for blk in range(4):
                      b = n * 4 + blk
                      tb = slice(blk * 128, (blk + 1) * 128)
                      Lt = gla_L(l, glT, tb)
                      pk = PS.get()
                      for kc in range(8):
                          mm(pk.ap[:, 0:256], hT[:, kc, tb], Wgk[:, kc, 0:256], kc == 0, kc == 7, [wn_gk, "hT"], pk.nm)
                      prb = PS.get()
                      mm(prb.ap[:, 0:256], TriB, Lt.ap[:, 0:256], True, True, [Lt.nm, "const"], prb.nm)
                      erb = SF.get()
                      act(erb.ap[:, 0:256], prb.ap[:, 0:256], AF.Exp, [prb.nm], [erb.nm])
                      khat = SBF.get()
                      tt(khat.ap[:, 0:256], pk.ap[:, 0:256], erb.ap[:, 0:256], ALU.mult, [pk.nm, erb.nm], [khat.nm])
                      for p in range(2):
                          mm(PS_X[:, p * 2:p * 2 + 2], Lt.ap[:, p * 128:(p + 1) * 128], TriA[:, 126:128], True, True, [Lt.nm, "const"], "ps_x")
                      act(Ebt[:].rearrange("p q c -> p (q c)"), PS_X[:, 0:4], AF.Exp, ["ps_x"], ["Ebt"])
                      Vt = SBF.get()
                      v_tok(Wv0, wn0, Wv1, wn1, tb, Vt.ap, Vt.nm)
                      pd = PS.get()
                      for p in range(2):
                          mm(pd.ap[:, p * 256:(p + 1) * 256], khat.ap[:, p * 128:(p + 1) * 128], Vt.ap[:, p * 256:(p + 1) * 256], True, True, [khat.nm, Vt.nm], pd.nm)
                      kb.op("dve", lambda e, b=b: e.tensor_copy(out=Sloc[:, b, :, :], in_=S_run[:]), reads=["S_run"], writes=["Sloc"])
                      for p in range(2):
                          for hp in range(2):
                              pr = slice(hp * 64, (hp + 1) * 64)
                              stt(S_run[pr, p, :], S_run[pr, p, :], Ebt[pr, p, 1:2], pd.ap[pr, p * 256 + hp * 128:p * 256 + (hp + 1) * 128],
                                  ALU.mult, ALU.add, ["S_run", "Ebt", pd.nm], ["S_run"])
                      tt(Ploc[:, b + 1, :], Ploc[:, b, :], Ebt[:, :, 1], ALU.mult, ["Ploc", "Ebt"], ["Ploc"])
                  if n == NT - 1:
                      Wskv, wn_s = W(l, "skv")
                      tl = slice(384, 512)
                      pkt = PS.get()
                      for kc in range(8):
                          mm(pkt.ap[:, 0:128], Wskv[:, kc, 0:128], hT[:, kc, tl], kc == 0, kc == 7, [wn_s, "hT"], pkt.nm)
                      tsc = SF.get()
                      act(tsc.ap[:, 0:128], pkt.ap[:, 0:128], AF.Identity, [pkt.nm], [tsc.nm])
                      pvt = PS.get()
                      for kc in range(8):
                          mm(pvt.ap[:, 0:128], hT[:, kc, tl], Wskv[:, kc, 128:256], kc == 0, kc == 7, [wn_s, "hT"], pvt.nm)
                      act(tsc.ap[:, 128:256], pvt.ap[:, 0:128], AF.Identity, [pvt.nm, tsc.nm], [tsc.nm])
                      put = PS.get()
                      for gg in range(2):
                          Wpu, wn_p = W(l, "pu%d" % gg)
                          for g2 in range(2):
                              g = gg * 2 + g2
                              for kc in range(8):
                                  mm(put.ap[:, g * 16:(g + 1) * 16], Wpu[:, kc, g2 * 128:(g2 + 1) * 128], hT[:, kc, 496:512], kc == 0, kc == 7, [wn_p, "hT"], put.nm)
                      act(tsc.ap[:, 256:320], put.ap[:, 0:64], AF.Identity, [put.nm, tsc.nm], [tsc.nm])
                      kb.dma("sp", "cc", ccA_in[l][:, 264:584], tsc.ap[:, 0:320], reads=[tsc.nm], writes=["ccAi"])
              if stop_after <= 1 and l == STOPL:
                  break
              kb.dma("sp", "cc", ccA_in[l][:, 0:256], S_run[:].rearrange("p q v -> p (q v)"), reads=["S_run"], writes=["ccAi"])
              kb.op("dve", lambda e: e.memset(At8[:], 0.0), reads=["At8"], writes=["At8"])
              kb.op("dve", lambda e: e.tensor_copy(out=At8[:, 0:2], in_=Ploc[:, NB, :]), reads=["Ploc", "At8"], writes=["At8"])
              kb.dma("sp", "cc", ccA_in[l][:, 256:264], At8[:], reads=["At8"], writes=["ccAi"])
              if phase == "A":
                  break
              if phase is None:
                  kb.wait_all("pool", ["ccAi"])
                  kb.wait_all("pool", ["ccAo"])
                  kb.barrier_dma("pool")
                  ins = nc.gpsimd.collective_compute("AllGather", ALU.bypass, replica_groups=[list(range(NCORES))],
                                                     ins=[ccA_in[l].ap().opt()], outs=[ccA_out[l].ap().opt()])
                  sem = kb.dma_sem("ccx")
                  kb.dcnt["ccx"] += 1
                  ins.then_inc(sem, 1)
                  kb._record((sem, kb.dcnt["ccx"], "dma", True), [], ["ccAo"])
                  kb.wait_all("pool", ["ccAo"])
              gsrc = ccA_out[l].ap().rearrange("(r p) f -> p r f", p=128)
              zflat = zbuf[:].rearrange("p m t -> p (m t)")
              gFA = zflat[:, 0:8 * 264].rearrange("p (r f) -> p r f", r=8)
              kb.dma("sp", "ld", gFA, gsrc[:, :, 0:264], reads=["ccAo"], writes=["zbuf"])
              kb.op("dve", lambda e: e.memset(S_in[:], 0.0), reads=["S_in"], writes=["S_in"])
              for c2 in range(NCORES):
                  mcol = pcs[:, c2:c2 + 1]
                  Fm = SF.get()
                  ts(Fm.ap[:, 0:256], gFA[:, c2, 0:256], mcol, None, ALU.mult, ALU.bypass, ["zbuf", "const"], [Fm.nm])
                  ts(Am[:], gFA[:, c2, 256:258], -1.0, mcol, ALU.add, ALU.mult, ["zbuf", "const"], ["Am"])
                  ts(Am[:], Am[:], 1.0, None, ALU.add, ALU.bypass, ["Am"], ["Am"])
                  for p in range(2):
                      stt(S_in[:, p, :], S_in[:, p, :], Am[:, p:p + 1], Fm.ap[:, p * 128:(p + 1) * 128], ALU.mult, ALU.add, ["S_in", "Am", Fm.nm], ["S_in"])
              for b in range(NB):
                  for p in range(2):
                      stt(Sloc[:, b, p, :], S_in[:, p, :], Ploc[:, b, p:p + 1], Sloc[:, b, p, :], ALU.mult, ALU.add, ["S_in", "Ploc", "Sloc"], ["Sloc"])
              gTL = zflat[:, 0:8 * 320].rearrange("p (r f) -> p r f", r=8)
              kb.dma("sp", "ld", gTL, gsrc[:, :, 264:584], reads=["ccAo"], writes=["zbuf"])
              ts(halo[:], gTL[:, 0, :], pcs[:, 8:9], None, ALU.mult, ALU.bypass, ["zbuf", "const"], ["halo"])
              for c2 in range(1, NCORES):
                  stt(halo[:], gTL[:, c2, :], pcs[:, 8 + c2:9 + c2], halo[:], ALU.mult, ALU.add, ["zbuf", "const", "halo"], ["halo"])
              for g in range(2):
                  kb.op("dve", lambda e, g=g: e.tensor_copy(out=ksp[g * 64:(g + 1) * 64, g, 0:128], in_=halo[g * 64:(g + 1) * 64, 0:128]), reads=["halo"], writes=["ksT"])
              for g in range(2):
                  kb.op("dve", lambda e, g=g: e.tensor_copy(out=Vpad[:, 0, g, g * 64:(g + 1) * 64], in_=halo[:, 128 + g * 64:128 + (g + 1) * 64]), reads=["halo"], writes=["Vpad"])
              kb.op("dve", lambda e: e.tensor_copy(out=uh[:].rearrange("p g t -> p (g t)"), in_=halo[:, 256:320]), reads=["halo"], writes=["uh"])

              if stop_after <= 2 and l == STOPL:
                  break
              for n in range(NT):
                  cols = slice(n * N, (n + 1) * N)
                  norm_tile(l, n, 0)
                  Wgk, wn_gk = W(l, "gkl")
                  glT = gla_glow(l, Wgk, wn_gk)
                  Wv0, wn0 = W(l, "gv0")
                  Wv1, wn1 = W(l, "gv1")
                  for blk in range(4):
                      tb = slice(blk * 128, (blk + 1) * 128)
                      Lt = gla_L(l, glT, tb)
                      pb = PS.get()
                      for p in range(2):
                          mm(pb.ap[:, p * 128:(p + 1) * 128], Lt.ap[:, p * 128:(p + 1) * 128], TriA, True, True, [Lt.nm, "const"], pb.nm)
                      act(ebT[:, :, tb], pb.ap[:, 0:256].rearrange("p (q t) -> p q t", q=2), AF.Exp, [pb.nm], ["ebT"])
                      act(enbT[:, :, tb], pb.ap[:, 0:256].rearrange("p (q t) -> p q t", q=2), AF.Exp, [pb.nm], ["enbT"], scale=-1.0)
                      v_tok(Wv0, wn0, Wv1, wn1, tb, Vg[:, blk, :], "Vg")
                  for p in range(2):
                      pk = PS.get()
                      for kc in range(8):
                          mm(pk.ap, Wgk[:, kc, p * 128:(p + 1) * 128], hT[:, kc, :], kc == 0, kc == 7, [wn_gk, "hT"], pk.nm)
                      tt(kgT[:, p, :], pk.ap, enbT[:, p, :], ALU.mult, [pk.nm, "enbT"], ["kgT"])
                  Wq, wn_q = W(l, "gq")
                  for p in range(2):
                      pq = PS.get()
                      for kc in range(8):
                          mm(pq.ap, Wq[:, kc, p * 128:(p + 1) * 128], hT[:, kc, :], kc == 0, kc == 7, [wn_q, "hT"], pq.nm)
                      for hp in range(2):
                          pr = slice(hp * 64, (hp + 1) * 64)
                          stt(qgp[pr, 2 * p + hp, :], pq.ap[pr, :], 0.125, ebT[pr, p, :], ALU.mult, ALU.mult, [pq.nm, "ebT"], ["qgT"])
                  for hh in range(2):
                      Wr, wn_r = W(l, "gr%d" % hh)
                      for h2 in range(2):
                          h = hh * 2 + h2
                          pr_ = PS.get()
                          for kc in range(8):
                              mm(pr_.ap, Wr[:, kc, h2 * 128:(h2 + 1) * 128], hT[:, kc, :], kc == 0, kc == 7, [wn_r, "hT"], pr_.nm)
                          act(srT[:, h, :], pr_.ap, AF.Silu, [pr_.nm], ["srT"])
                  cut(1)
                  for blk in range(4):
                      b = n * 4 + blk
                      tb = slice(blk * 128, (blk + 1) * 128)
                      pa = PS.get()
                      for h in range(4):
                          p, hp = divmod(h, 2)
                          pr = slice(hp * 64, (hp + 1) * 64)
                          mm(pa.ap[:, h * 128:(h + 1) * 128], kgT[:, p, tb], qgp[:, h, tb], True, True, ["kgT", "qgT"], pa.nm)
                      at = SBF.get()
                      tt(at.ap, pa.ap, maskc[:].rearrange("p j q -> p (j q)"), ALU.mult, [pa.nm, "const"], [at.nm])
                      po = PS.get()
                      for h in range(4):
                          p, hp = divmod(h, 2)
                          pr = slice(hp * 64, (hp + 1) * 64)
                          mm(po.ap[:, h * 128:(h + 1) * 128], Vg[:, blk, h * 128:(h + 1) * 128], at.ap[:, h * 128:(h + 1) * 128], True, False, ["Vg", at.nm], po.nm)
                          mm(po.ap[:, h * 128:(h + 1) * 128], Sloc[:, b, p, :], qgp[:, h, tb], False, True, ["Sloc", "qgT"], po.nm)
                      sq = SBF.get()
                      act(sq.ap, po.ap, AF.Square, [po.nm], [sq.nm])
                      mm(PS_ST, ones_bf[:, :], sq.ap, True, True, [sq.nm, "const"], "ps_st")
                      r = rstd_from(PS_ST, "ps_st", N, 1.0 / 128.0)
                      t1 = SF.get()
                      tt(t1.ap[:, 0:N].rearrange("p (h t) -> p h t", h=4), r.ap[:, 0:N].rearrange("p (h t) -> p h t", h=4), srT[:, :, tb], ALU.mult, [r.nm, "srT"], [t1.nm])
                      for h in range(4):
                          stt(glao[:, h, tb], po.ap[:, h * 128:(h + 1) * 128], cp(l, 32 + h), t1.ap[:, h * 128:(h + 1) * 128], ALU.mult, ALU.mult, [po.nm, t1.nm, "const"], ["glao"])
                  cut(2)
                  for jj in range(2):
                      Wsq, wn_sq = W(l, "sq%d" % jj)
                      for j2 in range(2):
                          j = jj * 2 + j2
                          pq = PS.get()
                          for kc in range(8):
                              mm(pq.ap, Wsq[:, kc, j2 * 128:(j2 + 1) * 128], hT[:, kc, :], kc == 0, kc == 7, [wn_sq, "hT"], pq.nm)
                          act(qsT[:, j, :], pq.ap, AF.Identity, [pq.nm], ["qsT"])
                  Wskv, wn_s = W(l, "skv")
                  pk = PS.get()
                  for kc in range(8):
                      mm(pk.ap, Wskv[:, kc, 0:128], hT[:, kc, :], kc == 0, kc == 7, [wn_s, "hT"], pk.nm)
                  for g in range(2):
                      act(ksp[g * 64:(g + 1) * 64, g, 128:640], pk.ap[g * 64:(g + 1) * 64, :], AF.Identity, [pk.nm], ["ksT"])
                  for blk in range(4):
                      tb = slice(blk * 128, (blk + 1) * 128)
                      pv = PS.get()
                      for kc in range(8):
                          mm(pv.ap[:, 0:128], hT[:, kc, tb], Wskv[:, kc, 128:256], kc == 0, kc == 7, [wn_s, "hT"], pv.nm)
                      for g in range(2):
                          act(Vpad[:, 1 + blk, g, g * 64:(g + 1) * 64], pv.ap[:, g * 64:(g + 1) * 64], AF.Identity, [pv.nm], ["Vpad"])
                  for blk in range(4):
                      b = n * 4 + blk
                      tb = slice(blk * 128, (blk + 1) * 128)
                      probs = []
                      for g in range(2):
                          pr = slice(g * 64, (g + 1) * 64)
                          for which in range(2):
                              kcols = slice(128 + blk * 128, 256 + blk * 128) if which == 0 else slice(blk * 128, 128 + blk * 128)
                              psc = PS.get()
                              mm(psc.ap.rearrange("p (j q) -> p j q", j=4), ksp[:, g, kcols], qsT[:, :, tb], True, True, ["ksT", "qsT"], psc.nm)
                              pe_ = SBF.get()
                              act(pe_.ap, psc.ap, AF.Exp, [psc.nm], [pe_.nm], scale=0.125)
                              msk = maskc if which == 0 else (maskp0 if b == 0 else maskp)
                              tt(pe_.ap, pe_.ap, msk[:].rearrange("p j q -> p (j q)"), ALU.mult, [pe_.nm, "const"], [pe_.nm])
                              probs.append((pe_, g, which))
                      po = PS.get()
                      pdn = PS.get()
                      for i, (pe_, g, which) in enumerate(probs):
                          vb = 1 + blk if which == 0 else blk
                          mm(po.ap, Vpad[:, vb, g, :], pe_.ap, i == 0, i == 3, ["Vpad", pe_.nm], po.nm)
                      for i, (pe_, g, which) in enumerate(probs):
                          mm(pdn.ap, onespad[:, g, :], pe_.ap, i == 0, i == 3, ["const", pe_.nm], pdn.nm)
                      den = SF.get()
                      for j in range(4):
                          ts(den.ap[:, j * 128:(j + 1) * 128], pdn.ap[:, j * 128:(j + 1) * 128], esink[:, l * 4 + j:l * 4 + j + 1], None, ALU.add, ALU.bypass, [pdn.nm, "const", den.nm], [den.nm])
                      kb.op("dve", lambda e, den=den: e.reciprocal(out=den.ap[:, 0:N], in_=den.ap[:, 0:N]), reads=[den.nm], writes=[den.nm])
                      tt(swao[:, :, tb], po.ap.rearrange("p (j q) -> p j q", j=4), den.ap[:, 0:N].rearrange("p (j q) -> p j q", j=4), ALU.mult, [po.nm, den.nm], ["swao"])
                  for g in range(2):
                      kb.op("dve", lambda e, g=g: e.tensor_copy(out=ksp[g * 64:(g + 1) * 64, g, 0:128], in_=ksp[g * 64:(g + 1) * 64, g, 512:640]), reads=["ksT"], writes=["ksT"])
                  for g in range(2):
                      kb.op("dve", lambda e, g=g: e.tensor_copy(out=Vpad[:, 0, g, g * 64:(g + 1) * 64], in_=Vpad[:, 4, g, g * 64:(g + 1) * 64]), reads=["Vpad"], writes=["Vpad"])
                  cut(3)
                  Wpw, wn_pw = W(l, "pw")
                  for gg in range(2):
                      Wpu, wn_p = W(l, "pu%d" % gg)
                      for g2 in range(2):
                          g = gg * 2 + g2
                          w = 2 ** (g + 1)
                          pu_ = PS.get()
                          for kc in range(8):
                              mm(pu_.ap, Wpu[:, kc, g2 * 128:(g2 + 1) * 128], hT[:, kc, :], kc == 0, kc == 7, [wn_p, "hT"], pu_.nm)
                          ub = SF.get()
                          act(ub.ap[:, 16:528], pu_.ap, AF.Identity, [pu_.nm], [ub.nm])
                          kb.op("dve", lambda e, ub=ub, g=g: e.tensor_copy(out=ub.ap[:, 0:16], in_=uh[:, g, :]), reads=["uh", ub.nm], writes=[ub.nm])
                          kb.op("dve", lambda e, ub=ub, g=g: e.tensor_copy(out=uh[:, g, :], in_=ub.ap[:, 512:528]), reads=[ub.nm, "uh"], writes=["uh"])
                          cur = ub
                          lo, sh = 0, 1
                          while sh < w:
                              nx = SF.get()
                              lo2 = lo + sh
                              tt(nx.ap[:, lo2:528], cur.ap[:, lo2:528], cur.ap[:, lo2 - sh:528 - sh], ALU.add, [cur.nm], [nx.nm])
                              cur, lo, sh = nx, lo2, sh * 2
                          dT = SBF.get()
                          stt(dT.ap, cur.ap[:, 16:528], 1.0 / w, ub.ap[:, 16:528], ALU.mult, ALU.subtract, [cur.nm, ub.nm], [dT.nm])
                          if n == 0:
                              tmp = SF.get()
                              tt(tmp.ap[:, 0:16], cur.ap[:, 16:32], pcs[:, 32 + g * 16:48 + g * 16], ALU.mult, [cur.nm, "const"], [tmp.nm])
                              tt(dT.ap[:, 0:16], tmp.ap[:, 0:16], ub.ap[:, 16:32], ALU.subtract, [tmp.nm, ub.nm, dT.nm], [dT.nm])
                          pp = PS.get()
                          mm(pp.ap, Wpw[:, g, :], dT.ap, True, True, [wn_pw, dT.nm], pp.nm)
                          ts(pcT[:, g, :], pp.ap, cp(l, 36 + g), None, ALU.mult, ALU.bypass, [pp.nm, "const"], ["pcT"])
                  cut(4)
                  for m in range(8):
                      Wgt, wn_g = W(l, "gt%d" % m)
                      Wbr, wn_b = W(l, "br%d" % m)
                      sigs = []
                      for bi in range(3):
                          pg = PS.get()
                          for kc in range(8):
                              mm(pg.ap, Wgt[:, kc, bi * 128:(bi + 1) * 128], hT[:, kc, :], kc == 0, kc == 7, [wn_g, "hT"], pg.nm)
                          sg = SBF.get()
                          act(sg.ap, pg.ap, AF.Sigmoid, [pg.nm], [sg.nm])
                          sigs.append(sg)
                      t0 = None
                      for bi, (src, sname) in enumerate(((glao, "glao"), (swao, "swao"), (pcT, "pcT"))):
                          py = PS.get()
                          for c in range(4):
                              mm(py.ap, Wbr[:, bi * 4 + c, :], src[:, c, :], c == 0, c == 3, [wn_b, sname], py.nm)
                          if bi == 0:
                              t0 = SF.get()
                              tt(t0.ap[:, 0:N], py.ap, sigs[0].ap, ALU.mult, [py.nm, sigs[0].nm], [t0.nm])
                          else:
                              t1 = SF.get()
                              tt(t1.ap[:, 0:N], py.ap, sigs[bi].ap, ALU.mult, [py.nm, sigs[bi].nm], [t1.nm])
                              if bi == 1:
                                  tt(t0.ap[:, 0:N], t0.ap[:, 0:N], t1.ap[:, 0:N], ALU.add, [t0.nm, t1.nm], [t0.nm])
                              else:
                                  tt(mrg[:, m, :], t0.ap[:, 0:N], t1.ap[:, 0:N], ALU.add, [t0.nm, t1.nm], ["mrg"])
                  cut(5)
                  for m2 in range(8):
                      Wo, wn_o = W(l, "wo%d" % m2)
                      pz = PS.get()
                      for m in range(8):
                          mm(pz.ap, Wo[:, m, :], mrg[:, m, :], m == 0, m == 7, [wn_o, "mrg"], pz.nm)
                      act(zbuf[:, m2, :], pz.ap, AF.Identity, [pz.nm], ["zbuf"])
                      sq = SBF.get()
                      act(sq.ap, pz.ap, AF.Square, [pz.nm], [sq.nm])
                      mm(PS_ST, ones_bf[:, :], sq.ap, m2 == 0, m2 == 7, [sq.nm, "const"], "ps_st")
                  post_norm_residual(l, n, 8)

          if stop_after <= 3 and l == STOPL:
              break
          if phase == "B":
              break
          kb.barrier()
          if phase is None:
              kb.op("dve", lambda e: e.tensor_copy(out=xt2[:], in_=xT[:, :, TC - 2:TC]), reads=["xT", "xt2"], writes=["xt2"])
              kb.dma("sp", "cc", ccB_in[l][:, :], xt2[:].rearrange("p k t -> p (k t)"), reads=["xt2"], writes=["ccBi"])
              kb.wait_all("pool", ["ccBi"])
              kb.wait_all("pool", ["ccBo"])
              kb.barrier_dma("pool")
              ins = nc.gpsimd.collective_compute("AllGather", ALU.bypass, replica_groups=[list(range(NCORES))],
                                                 ins=[ccB_in[l].ap().opt()], outs=[ccB_out[l].ap().opt()])
              sem = kb.dma_sem("ccx")
              kb.dcnt["ccx"] += 1
              ins.then_inc(sem, 1)
              kb._record((sem, kb.dcnt["ccx"], "dma", True), [], ["ccBo"])
              kb.wait_all("pool", ["ccBo"])
          gB = zbuf[:].rearrange("p m t -> p (m t)")[:, 0:8 * XB].rearrange("p (r f) -> p r f", r=8)
          kb.dma("sp", "ld", gB, ccB_out[l].ap().rearrange("(r p) f -> p r f", p=128), reads=["ccBo"], writes=["zbuf"])
          ts(xh[:], gB[:, 0, :], pcs[:, 8:9], None, ALU.mult, ALU.bypass, ["zbuf", "const"], ["xh"])
          for c2 in range(1, NCORES):
              stt(xh[:], gB[:, c2, :], pcs[:, 8 + c2:9 + c2], xh[:], ALU.mult, ALU.add, ["zbuf", "const", "xh"], ["xh"])
          xh3 = xh[:].rearrange("p (k t) -> p k t", k=8)
          sqh = SBF.get()
          act(sqh.ap[:, 0:16], xh[:], AF.Square, ["xh"], [sqh.nm])
          for kc in range(8):
              mm(PS_X[:, 8:10], ones_bf[:, :], sqh.ap[:, kc * 2:kc * 2 + 2], kc == 0, kc == 7, [sqh.nm, "const"], "ps_x")
          rh = rstd_from(PS_X[:, 8:10], "ps_x", 2, 1.0 / D)
          for kc in range(8):
              stt(h2h[:, kc, :], xh3[:, kc, :], cp(l, 16 + kc), rh.ap[:, 0:2], ALU.mult, ALU.mult, ["xh", rh.nm, "const"], ["h2h"])

          if stop_after <= 4 and l == STOPL:
              break
          for n in range(NT):
              norm_tile(l, n, 16)
              for j in range(NFC):
                  Wu, wn_u = W(l, "up%d" % j)
                  gact = None
                  for half in range(2):
                      cidx = j + NFC * half
                      pu_ = PS.get()
                      for kc in range(8):
                          mm(pu_.ap, Wu[:, kc, half * 128:(half + 1) * 128], hT[:, kc, :], kc == 0, kc == 7, [wn_u, "hT"], pu_.nm)
                      U = SF.get()
                      act(U.ap[:, 2:514], pu_.ap, AF.Identity, [pu_.nm], [U.nm])
                      if n == 0:
                          for kc in range(8):
                              mm(PS_X[:, 16:18], Wu[:, kc, half * 128:(half + 1) * 128], h2h[:, kc, :], kc == 0, kc == 7, [wn_u, "h2h"], "ps_x")
                          act(U.ap[:, 0:2], PS_X[:, 16:18], AF.Identity, ["ps_x", U.nm], [U.nm])
                      else:
                          kb.op("dve", lambda e, U=U, cidx=cidx: e.tensor_copy(out=U.ap[:, 0:2], in_=uph[:, cidx, :]), reads=["uph", U.nm], writes=[U.nm])
                      kb.op("dve", lambda e, U=U, cidx=cidx: e.tensor_copy(out=uph[:, cidx, :], in_=U.ap[:, 512:514]), reads=[U.nm, "uph"], writes=["uph"])
                      acc = SF.get()
                      ts(acc.ap[:, 0:N], U.ap[:, 2:514], cp(l, 40 + 2 * 44 + cidx), cp(l, 172 + cidx), ALU.mult, ALU.add, [U.nm, "const"], [acc.nm])
                      stt(acc.ap[:, 0:N], U.ap[:, 1:513], cp(l, 40 + 1 * 44 + cidx), acc.ap[:, 0:N], ALU.mult, ALU.add, [U.nm, acc.nm, "const"], [acc.nm])
                      stt(acc.ap[:, 0:N], U.ap[:, 0:512], cp(l, 40 + cidx), acc.ap[:, 0:N], ALU.mult, ALU.add, [U.nm, acc.nm, "const"], [acc.nm])
                      if half == 0:
                          gact = SF.get()
                          act(gact.ap[:, 0:N], acc.ap[:, 0:N], AF.Gelu_apprx_tanh, [acc.nm], [gact.nm])
                      else:
                          tt(gT[:, j, :], gact.ap[:, 0:N], acc.ap[:, 0:N], ALU.mult, [gact.nm, acc.nm], ["gT"])
              for m in range(8):
                  Wd, wn_d = W(l, "dn%d" % m)
                  pz = PS.get()
                  for fc in range(NFC):
                      mm(pz.ap, Wd[:, fc, :], gT[:, fc, :], fc == 0, fc == NFC - 1, [wn_d, "gT"], pz.nm)
                  act(zbuf[:, m, :], pz.ap, AF.Identity, [pz.nm], ["zbuf"])
                  sq = SBF.get()
                  act(sq.ap, pz.ap, AF.Square, [pz.nm], [sq.nm])
                  mm(PS_ST, ones_bf[:, :], sq.ap, m == 0, m == 7, [sq.nm, "const"], "ps_st")
              post_norm_residual(l, n, 24)
          kb.barrier()

    try:
        body()
    except StopBuild:
        pass
    if phase == "A":
        kb.wait_all("sp", ["ccAi"])
    else:
        kb.dma("sp", "st", yT_d, xT[:], reads=["xT"], writes=["yT"])
        kb.wait_all("sp", ["yT"])
    kb.barrier()
    for cm in reversed(ctxs):
        cm.__exit__(None, None, None)
    kb.close()
    return nc, kb


_CACHE = {}
FUSED = False


def _launch(inp, depth, stop_after, xTs, phase=None, extra=None, outs=("yT",)):
    wpack = np.stack([_pack_weights(inp, l) for l in range(depth)], axis=0)
    if os.environ.get("KDBG_SMALLW"):
        wpack = np.ascontiguousarray(wpack[:, :128 * SLOT])
    cpack = _pack_small(inp)
    wgb = np.zeros((17, DEPTH * 256), np.float32)
    for l in range(DEPTH):
        wgb[0:16, l * 256:(l + 1) * 256] = inp["gla_w_gate_up"][l]
        wgb[16, l * 256:(l + 1) * 256] = inp["gla_b_gate"][l]
    in_maps = []
    for c in range(NCORES):
        cst, pc = _consts(c)
        pad = float(c) if phase is None else 0.0
        wpc = np.concatenate([wpack, np.full((1, wpack.shape[1]), pad, np.float32)], axis=0)
        m = {"xT": xTs[c], "wpack": wpc, "cpack": cpack, "wgb": wgb, "cst": cst, "pcst": pc}
        if extra:
            m.update(extra)
        in_maps.append(m)
    key = (depth, stop_after, phase)
    if key not in _CACHE:
        _CACHE[key] = build(depth, stop_after, phase)[0]
    res = run_bass_kernel_spmd(_CACHE[key], in_maps, core_ids=list(range(NCORES)))
    if os.environ.get("KDBG_VERBOSE"):
        print("launch done depth", depth, phase, flush=True)
    return [[np.asarray(res.results[c][o]) for c in range(NCORES)] for o in outs]


def kernel(**inp):
    inp = {k: np.asarray(v) for k, v in inp.items()}
    depth = int(inp.pop("_depth", DEPTH)) if "_depth" in inp else DEPTH
    stop_after = int(inp.pop("_stop", 99)) if "_stop" in inp else 99
    fused = bool(inp.pop("_fused", FUSED)) if "_fused" in inp else FUSED
    x = inp["x"][0]
    xTs = []
    for c in range(NCORES):
        xs = x[c * TC:(c + 1) * TC]
        xTs.append(np.ascontiguousarray(xs.T.reshape(8, 128, TC).transpose(1, 0, 2)))
    if fused:
        xTs = _launch(inp, depth, stop_after, xTs)[0]
    else:
        for l in range(depth):
            inp_l = {k: (v if k == "x" else np.repeat(v[l:l + 1], DEPTH, axis=0)) for k, v in inp.items()}
            payA = _launch(inp_l, 1, 99, xTs, phase="A", outs=("payA",))[0]
            gA = np.ascontiguousarray(np.concatenate(payA, axis=0))
            xTs = _launch(inp_l, 1, 99, xTs, phase="B", extra={"gA": gA})[0]
            gB = np.ascontiguousarray(np.concatenate([t[:, :, TC - 2:TC].reshape(128, XB) for t in xTs], axis=0))
            xTs = _launch(inp_l, 1, 99, xTs, phase="C", extra={"gB": gB})[0]
    out = np.empty((1, T_ALL, D), np.float32)
    for c in range(NCORES):
        out[0, c * TC:(c + 1) * TC] = xTs[c].transpose(1, 0, 2).reshape(D, TC).T
    return out
```

```python
import os
import numpy as np
import concourse.bass as bass
import concourse.mybir as mybir
from concourse.bass_utils import run_bass_kernel_spmd

F32 = mybir.dt.float32
BF16 = mybir.dt.bfloat16
AF = mybir.ActivationFunctionType
ALU = mybir.AluOpType

NCORES = 8
T_ALL = 16384
TC = 2048
D = 1024
DEPTH = 4
N = 512
NT = TC // N
NB = TC // 128
DFF = 2816
NFC = DFF // 128
EPS = 1e-6
O_GQ, O_GK, O_GV, O_GLOW, O_GR, O_SQ, O_SK, O_SV, O_PU, O_GATE = 0, 256, 512, 1024, 1040, 1552, 2064, 2192, 2320, 2832
CP = 220
XA = 584
XB = 16
SLOT = 3072
NSLOT = 4


class KB:
    def __init__(self, nc):
        self.nc = nc
        self.engs = {"pe": nc.tensor, "act": nc.scalar, "dve": nc.vector, "pool": nc.gpsimd, "sp": nc.sync}
        self.esem, self.ecnt, self._ctx = {}, {}, []
        for n in ("pe", "act", "dve", "pool"):
            cm = nc.semaphore("es_" + n)
            self.esem[n] = cm.__enter__()
            self._ctx.append(cm)
            self.ecnt[n] = 0
        self.known = {n: {} for n in self.engs}
        self.lastw, self.readers = {}, {}
        self.dsem, self.dcnt = {}, {}
        self.nwaits = 0
        self.nins = 0

    def dma_sem(self, key):
        if key not in self.dsem:
            cm = self.nc.semaphore("ds_" + str(key))
            self.dsem[key] = cm.__enter__()
            self._ctx.append(cm)
            self.dcnt[key] = 0
        return self.dsem[key]

    def close(self):
        for cm in reversed(self._ctx):
            cm.__exit__(None, None, None)

    def _wait(self, en, sem, val):
        k = self.known[en]
        key = sem.name
        if k.get(key, 0) >= val:
            return
        k[key] = val
        self.engs[en].wait_ge(sem, val)
        self.nwaits += 1

    def _deps(self, en, reads, writes):
        need = []
        for b in reads:
            w = self.lastw.get(b)
            if w is not None and not (en == "pe" and w[2] == "pe"):
                need.append(w)
        for b in writes:
            w = self.lastw.get(b)
            if w is not None and not (en == "pe" and w[2] == "pe"):
                need.append(w)
            for r in self.readers.get(b, ()):
                if not (en == "pe" and r[2] == "pe"):
                    need.append(r)
        for ev in need:
            self._wait(en, ev[0], ev[1])

    def _record(self, ev, reads, writes):
        for b in writes:
            self.lastw[b] = ev
            self.readers[b] = []
        for b in reads:
            lst = self.readers.setdefault(b, [])
            for i, r in enumerate(lst):
                if r[0] is ev[0]:
                    lst[i] = ev
                    break
            else:
                lst.append(ev)

    def op(self, en, fn, reads=(), writes=()):
        reads = [getattr(r, "nm", r) for r in reads]
        writes = [getattr(r, "nm", r) for r in writes]
        self._deps(en, reads, writes)
        ins = fn(self.engs[en])
        self.ecnt[en] += 1
        self.nins += 1
        ins.then_inc(self.esem[en], 1)
        self._record((self.esem[en], self.ecnt[en], en, False), reads, writes)
        return ins

    def dma(self, q, key, out, in_, reads=(), writes=()):
        reads = [getattr(r, "nm", r) for r in reads]
        writes = [getattr(r, "nm", r) for r in writes]
        sem = self.dma_sem(key)
        self._deps(q, reads, writes)
        ins = self.engs[q].dma_start(out=out, in_=in_)
        self.dcnt[key] += 16
        self.nins += 1
        ins.then_inc(sem, 16)
        self._record((sem, self.dcnt[key], "dma", True), reads, writes)
        return ins

    def wait_all(self, en, bufs):
        for b in bufs:
            w = self.lastw.get(b)
            if w is not None:
                self._wait(en, w[0], w[1])

    def barrier_dma(self, en):
        for key, sem in self.dsem.items():
            if self.dcnt[key] > 0:
                self._wait(en, sem, self.dcnt[key])

    def barrier(self):
        for en in ("pe", "act", "dve", "pool", "sp"):
            for n2 in ("pe", "act", "dve", "pool"):
                if n2 != en and self.ecnt[n2] > 0:
                    self._wait(en, self.esem[n2], self.ecnt[n2])
            for key, sem in self.dsem.items():
                if self.dcnt[key] > 0:
                    self._wait(en, sem, self.dcnt[key])


class StopBuild(Exception):
    pass


class Rot:
    def __init__(self, name, aps):
        self.name, self.aps, self.i = name, aps, 0
        self.gen = [0] * len(aps)

    def get(self):
        i = self.i % len(self.aps)
        self.i += 1
        self.gen[i] += 1
        return RB(self, i, self.gen[i])


class RB:
    def __init__(self, rot, i, gen):
        self.rot, self.i, self.gen = rot, i, gen
        self.ap = rot.aps[i]
        self.n = "%s%d" % (rot.name, i)

    @property
    def nm(self):
        assert self.rot.gen[self.i] == self.gen, "stale rotating buffer %s" % self.n
        return self.n


def _wspec():
    s = {}
    s["gkl"] = (8, 272)
    s["gq"] = (8, 256)
    s["gv0"] = (8, 256); s["gv1"] = (8, 256)
    s["gr0"] = (8, 256); s["gr1"] = (8, 256)
    s["sq0"] = (8, 256); s["sq1"] = (8, 256)
    s["skv"] = (8, 256)
    s["pu0"] = (8, 256); s["pu1"] = (8, 256)
    s["pw"] = (4, 128)
    for m in range(8):
        s["gt%d" % m] = (8, 384)
        s["br%d" % m] = (12, 128)
        s["wo%d" % m] = (8, 128)
        s["dn%d" % m] = (NFC, 128)
    for j in range(NFC):
        s["up%d" % j] = (8, 256)
    off, o = {}, 0
    for k, (nk, wc) in s.items():
        off[k] = o
        o += 128 * nk * wc
    return s, off, o


WSPEC, WOFF, WTOT = _wspec()


def _img(w):
    K, wc = w.shape
    nk = K // 128
    return np.ascontiguousarray(w.reshape(nk, 128, wc).transpose(1, 0, 2)).reshape(128, nk * wc)


def _pack_weights(inp, l):
    w_in = inp["w_in"][l]
    out = np.empty((WTOT,), np.float32)

    def put(name, mat):
        nk, wc = WSPEC[name]
        assert mat.shape == (nk * 128, wc), (name, mat.shape)
        out[WOFF[name]:WOFF[name] + 128 * nk * wc] = _img(mat).reshape(-1)

    put("gkl", np.concatenate([w_in[:, O_GK:O_GK + 256], w_in[:, O_GLOW:O_GLOW + 16]], axis=1))
    put("gq", w_in[:, O_GQ:O_GQ + 256])
    put("gv0", w_in[:, O_GV:O_GV + 256]); put("gv1", w_in[:, O_GV + 256:O_GV + 512])
    put("gr0", w_in[:, O_GR:O_GR + 256]); put("gr1", w_in[:, O_GR + 256:O_GR + 512])
    sqc = []
    for j in range(4):
        for g in range(2):
            h = g * 4 + j
            sqc.append(w_in[:, O_SQ + h * 64:O_SQ + (h + 1) * 64])
    sqc = np.concatenate(sqc, axis=1)
    put("sq0", sqc[:, :256]); put("sq1", sqc[:, 256:])
    put("skv", w_in[:, O_SK:O_SK + 256])
    put("pu0", w_in[:, O_PU:O_PU + 256]); put("pu1", w_in[:, O_PU + 256:O_PU + 512])
    pw = inp["pool_w"][l]
    put("pw", pw.reshape(4 * 128, 128))
    wg_, ws_, wp_ = inp["w_branch_gla"][l], inp["w_branch_swa"][l], inp["w_branch_pool"][l]
    perm = np.array([g * 256 + j * 64 + d for j in range(4) for g in range(2) for d in range(64)])
    ws_p = ws_[perm]
    wo_ = inp["w_out"][l]
    wup, wdn = inp["ffn_w_up"][l], inp["ffn_w_down"][l]
    for m in range(8):
        cs = slice(m * 128, (m + 1) * 128)
        put("gt%d" % m, np.concatenate([w_in[:, O_GATE + b * 1024 + m * 128:O_GATE + b * 1024 + (m + 1) * 128] for b in range(3)], axis=1))
        put("br%d" % m, np.concatenate([wg_[:, cs], ws_p[:, cs], wp_[:, cs]], axis=0))
        put("wo%d" % m, wo_[:, cs])
        put("dn%d" % m, wdn[:, cs])
    for j in range(NFC):
        put("up%d" % j, np.concatenate([wup[:, j * 128:(j + 1) * 128], wup[:, DFF + j * 128:DFF + (j + 1) * 128]], axis=1))
    return out


def _pack_small(inp):
    cp = np.zeros((128, DEPTH * CP), np.float32)
    for l in range(DEPTH):
        b = l * CP
        for i, k in enumerate(["norm_mix_pre", "norm_mix_post", "norm_ffn_pre", "norm_ffn_post"]):
            cp[:, b + i * 8:b + (i + 1) * 8] = inp[k][l].reshape(8, 128).T
        cp[:, b + 32:b + 36] = inp["gla_norm"][l].reshape(4, 128).T
        cp[:, b + 36:b + 40] = inp["pool_scale"][l].reshape(4, 128).T
        cw = inp["ffn_conv_w"][l]
        for i in range(3):
            cp[:, b + 40 + i * 44:b + 40 + (i + 1) * 44] = cw[i].reshape(44, 128).T
        cp[:, b + 172:b + 216] = inp["ffn_conv_b"][l].reshape(44, 128).T
        sk = inp["swa_sinks"][l]
        cp[0:64, b + 216:b + 220] = sk[0:4][None, :]
        cp[64:128, b + 216:b + 220] = sk[4:8][None, :]
    return cp


def _consts(core):
    j = np.arange(128)[:, None]
    i = np.arange(128)[None, :]
    cst = np.zeros((128, 384), np.float32)
    cst[:, 0:128] = np.where(j <= i, -1.0 / 16.0, 0.0)
    cst[:, 128:256] = np.where(j > i, -1.0 / 16.0, 0.0)
    cst[:, 256:384] = np.where(j <= i, 1.0, 0.0)
    pc = np.zeros((128, 96), np.float32)
    for c2 in range(8):
        pc[:, c2] = 1.0 if c2 < core else 0.0
        pc[:, 8 + c2] = 1.0 if c2 == core - 1 else 0.0
    pc[:, 16] = 1.0 if core > 0 else 0.0
    for g in range(4):
        w = 2 ** (g + 1)
        for t in range(16):
            pos = core * TC + t + 1
            pc[:, 32 + g * 16 + t] = 1.0 / min(pos, w)
    return cst, pc


def build(depth=DEPTH, stop_after=99, phase=None):
    nc = bass.Bass("TRN2", target_bir_lowering=False)
    xT_d = nc.dram_tensor("xT", [128, 8, TC], F32, kind="ExternalInput").ap()
    SMALLW = bool(os.environ.get("KDBG_SMALLW"))
    wp_d = nc.dram_tensor("wpack", [depth + 1, 128 * SLOT if SMALLW else WTOT], F32, kind="ExternalInput").ap()
    cp_d = nc.dram_tensor("cpack", [128, DEPTH * CP], F32, kind="ExternalInput").ap()
    wgb_d = nc.dram_tensor("wgb", [17, DEPTH * 256], F32, kind="ExternalInput").ap()
    cst_d = nc.dram_tensor("cst", [128, 384], F32, kind="ExternalInput").ap()
    pc_d = nc.dram_tensor("pcst", [128, 96], F32, kind="ExternalInput").ap()
    yT_d = nc.dram_tensor("yT", [128, 8, TC], F32, kind="ExternalOutput").ap() if phase != "A" else None
    if phase == "A":
        ccA_in = [nc.dram_tensor("payA", [128, XA], F32, kind="ExternalOutput")]
    else:
        ccA_in = [nc.dram_tensor("ccAi%d" % l, [128, XA], F32) for l in range(depth)]
    if phase == "B":
        ccA_out = [nc.dram_tensor("gA", [NCORES * 128, XA], F32, kind="ExternalInput")]
    else:
        ccA_out = [nc.dram_tensor("ccAo%d" % l, [NCORES * 128, XA], F32) for l in range(depth)]
    ccB_in = [nc.dram_tensor("ccBi%d" % l, [128, XB], F32) for l in range(depth)]
    if phase == "C":
        ccB_out = [nc.dram_tensor("gB", [NCORES * 128, XB], F32, kind="ExternalInput")]
    else:
        ccB_out = [nc.dram_tensor("ccBo%d" % l, [NCORES * 128, XB], F32) for l in range(depth)]
    dbg_outs = {}

    kb = KB(nc)
    ctxs = []

    def sb(name, shape, dt):
        cm = nc.sbuf_tensor(name, shape, dt)
        t = cm.__enter__()
        ctxs.append(cm)
        return t

    xT = sb("xTs", [128, 8, TC], F32)
    wsl = sb("wsl", [128, NSLOT, SLOT], BF16)
    cpk = sb("cpk", [128, DEPTH * CP], F32)
    wgb = sb("wgbs", [17, DEPTH * 256], F32)
    cst = sb("csts", [128, 384], F32)
    pcs = sb("pcs", [128, 96], F32)
    maskc = sb("maskc", [128, 4, 128], BF16)
    maskp = sb("maskp", [128, 4, 128], BF16)
    maskp0 = sb("maskp0", [128, 4, 128], BF16)
    ones_bf = sb("ones_bf", [128, 128], BF16)
    onespad = sb("onespad", [128, 2, 128], BF16)
    esink = sb("esink", [128, DEPTH * 4], F32)
    hT = sb("hT", [128, 8, N], BF16)
    zbuf = sb("zbuf", [128, 8, N], F32)
    S_run = sb("S_run", [128, 2, 128], F32)
    S_in = sb("S_in", [128, 2, 128], F32)
    Ploc = sb("Ploc", [128, NB + 1, 2], F32)
    Ebt = sb("Ebt", [128, 2, 2], F32)
    Am = sb("Am", [128, 2], F32)
    At8 = sb("At8", [128, 8], F32)
    halo = sb("halo", [128, 320], F32)
    uh = sb("uh", [128, 4, 16], F32)
    uph = sb("uph", [128, 44, 2], F32)
    xh = sb("xh", [128, 16], F32)
    glTt = sb("glTt", [32, N], F32)
    xt2 = sb("xt2", [128, 8, 2], F32)
    RB_EL = 28304
    Rb = sb("Rb", [128, RB_EL], BF16)
    o = [0]

    def carve(n):
        a = o[0]
        o[0] += n
        return a

    o[0] = 0
    a_Sloc = carve(NB * 256); a_eb = carve(1024); a_enb = carve(1024); a_Vg = carve(2048)
    a_kg = carve(1024); a_sr = carve(2048); a_glao = carve(2048)
    a_qs = carve(2048); a_swao = carve(2048); a_pc = carve(2048); a_mrg = carve(4096)
    a_qg = carve(2048); a_ks = carve(1280); a_vp = carve(1280); a_h2h = carve(16)
    assert o[0] <= RB_EL, o[0]
    assert a_qg >= NFC * N
    Sloc = Rb[:, a_Sloc:a_Sloc + NB * 256].rearrange("p (b q v) -> p b q v", b=NB, q=2)
    ebT = Rb[:, a_eb:a_eb + 1024].rearrange("p (q t) -> p q t", q=2)
    enbT = Rb[:, a_enb:a_enb + 1024].rearrange("p (q t) -> p q t", q=2)
    Vg = Rb[:, a_Vg:a_Vg + 2048].rearrange("p (b v) -> p b v", b=4)
    kgT = Rb[:, a_kg:a_kg + 1024].rearrange("p (q t) -> p q t", q=2)
    qgp = Rb[:, a_qg:a_qg + 2048].rearrange("p (h t) -> p h t", h=4)
    srT = Rb[:, a_sr:a_sr + 2048].rearrange("p (h t) -> p h t", h=4)
    glao = Rb[:, a_glao:a_glao + 2048].rearrange("p (h t) -> p h t", h=4)
    qsT = Rb[:, a_qs:a_qs + 2048].rearrange("p (j t) -> p j t", j=4)
    swao = Rb[:, a_swao:a_swao + 2048].rearrange("p (j t) -> p j t", j=4)
    pcT = Rb[:, a_pc:a_pc + 2048].rearrange("p (g t) -> p g t", g=4)
    mrg = Rb[:, a_mrg:a_mrg + 4096].rearrange("p (m t) -> p m t", m=8)
    ksp = Rb[:, a_ks:a_ks + 1280].rearrange("p (g t) -> p g t", g=2)
    Vpad = Rb[:, a_vp:a_vp + 1280].rearrange("p (b g d) -> p b g d", b=5, g=2)
    h2h = Rb[:, a_h2h:a_h2h + 16].rearrange("p (k t) -> p k t", k=8)
    assert a_ks >= NFC * N
    gT = Rb[:, 0:NFC * N].rearrange("p (f t) -> p f t", f=NFC)
    NSF, NSB = 6, 6
    sf_t = sb("scrf", [128, NSF, 528], F32)
    sbf_t = sb("scrb", [128, NSB, 512], BF16)
    SF = Rot("sf", [sf_t[:, i, :] for i in range(NSF)])
    SBF = Rot("sb", [sbf_t[:, i, :] for i in range(NSB)])
    cmps = nc.psum_tensor("ps", [128, 8, 512], F32)
    ps_t = cmps.__enter__()
    ctxs.append(cmps)
    PS = Rot("ps", [ps_t[:, i, :] for i in range(6)])
    PS_ST = ps_t[:, 6, :]
    PS_X = ps_t[:, 7, :]

    WS = Rot("w", [wsl[:, i, :] for i in range(NSLOT)])

    def W(l, name):
        nk, wc = WSPEC[name]
        rb = WS.get()
        n = nk * wc
        woff = 0 if SMALLW else WOFF[name]
        src = wp_d[l, woff:woff + 128 * n].rearrange("(p n) -> p n", p=128)
        kb.dma("pool", rb.n, rb.ap[:, 0:n], src, writes=[rb])
        return rb.ap[:, 0:n].rearrange("p (k c) -> p k c", k=nk), rb

    def mm(out, lhsT, rhs, start, stop, reads, wname):
        kb.op("pe", lambda e: e.matmul(out, lhsT=lhsT, rhs=rhs, start=start, stop=stop), reads=reads, writes=[wname])

    def act(out, in_, func, reads, writes, **kw):
        kb.op("act", lambda e: e.activation(out=out, in_=in_, func=func, **kw), reads=reads, writes=writes)

    def tt(out, in0, in1, op, reads, writes, en="dve"):
        kb.op(en, lambda e: e.tensor_tensor(out=out, in0=in0, in1=in1, op=op), reads=reads, writes=writes)

    def ts(out, in0, s1, s2, op0, op1, reads, writes, en="dve"):
        kb.op(en, lambda e: e.tensor_scalar(out=out, in0=in0, scalar1=s1, scalar2=s2, op0=op0, op1=op1), reads=reads, writes=writes)

    def stt(out, in0, scalar, in1, op0, op1, reads, writes, en="dve"):
        kb.op(en, lambda e: e.scalar_tensor_tensor(out=out, in0=in0, scalar=scalar, in1=in1, op0=op0, op1=op1), reads=reads, writes=writes)

    def cp(l, off, n=1):
        return cpk[:, l * CP + off:l * CP + off + n]

    def rstd_from(ps_ap, ps_name, ncols, inv_n):
        lnv = SF.get()
        act(lnv.ap[:, 0:ncols], ps_ap, AF.Ln, [ps_name], [lnv.nm], scale=inv_n, bias=EPS)
        r = SF.get()
        act(r.ap[:, 0:ncols], lnv.ap[:, 0:ncols], AF.Exp, [lnv.nm], [r.nm], scale=-0.5)
        return r

    def norm_tile(l, n, goff):
        cols = slice(n * N, (n + 1) * N)
        for kc in range(8):
            sq = SBF.get()
            act(sq.ap, xT[:, kc, cols], AF.Square, ["xT"], [sq.nm])
            mm(PS_ST, ones_bf[:, :], sq.ap, kc == 0, kc == 7, [sq.nm, "const"], "ps_st")
        r = rstd_from(PS_ST, "ps_st", N, 1.0 / D)
        for kc in range(8):
            stt(hT[:, kc, :], xT[:, kc, cols], cp(l, goff + kc), r.ap[:, 0:N], ALU.mult, ALU.mult, ["xT", r.nm, "const"], ["hT"])

    def post_norm_residual(l, n, goff):
        cols = slice(n * N, (n + 1) * N)
        r = rstd_from(PS_ST, "ps_st", N, 1.0 / D)
        for m in range(8):
            stt(zbuf[:, m, :], zbuf[:, m, :], cp(l, goff + m), r.ap[:, 0:N], ALU.mult, ALU.mult, ["zbuf", r.nm, "const"], ["zbuf"])
            tt(xT[:, m, cols], xT[:, m, cols], zbuf[:, m, :], ALU.add, ["xT", "zbuf"], ["xT"])

    def gla_L(l, glT, tb):
        pl = PS.get()
        mm(pl.ap[:, 0:256], glT[0:17, tb], wgb[0:17, l * 256:(l + 1) * 256], True, True, ["glT", "const"], pl.nm)
        e1 = SF.get()
        act(e1.ap[:, 0:256], pl.ap[:, 0:256], AF.Exp, [pl.nm], [e1.nm], scale=-1.0)
        Lt = SF.get()
        act(Lt.ap[:, 0:256], e1.ap[:, 0:256], AF.Ln, [e1.nm], [Lt.nm], bias=1.0)
        return Lt

    def gla_glow(l, Wgk, wn):
        pg = PS.get()
        for kc in range(8):
            mm(pg.ap[0:16, :], Wgk[:, kc, 256:272], hT[:, kc, :], kc == 0, kc == 7, [wn, "hT"], pg.nm)
        kb.op("dve", lambda e: e.memset(glTt[:], 1.0), reads=["glT"], writes=["glT"])
        act(glTt[0:16, :], pg.ap[0:16, :], AF.Identity, [pg.nm, "glT"], ["glT"])
        return glTt

    def v_tok(Wv0, wn0, Wv1, wn1, tb, out_ap, out_name):
        pv = PS.get()
        for half, (Wv, wn) in enumerate(((Wv0, wn0), (Wv1, wn1))):
            for kc in range(8):
                mm(pv.ap[:, half * 256:(half + 1) * 256], hT[:, kc, tb], Wv[:, kc, :], kc == 0, kc == 7, [wn, "hT"], pv.nm)
        act(out_ap, pv.ap, AF.Identity, [pv.nm], [out_name])

    kb.dma("sp", "ld", xT[:], xT_d, writes=["xT"])
    kb.dma("sp", "ld", cpk[:], cp_d, writes=["const"])
    kb.dma("sp", "ld", wgb[:], wgb_d, writes=["const"])
    kb.dma("sp", "ld", cst[:], cst_d, writes=["const"])
    kb.dma("sp", "ld", pcs[:], pc_d, writes=["const"])
    TriA, TriB, cmask = cst[:, 0:128], cst[:, 128:256], cst[:, 256:384]
    kb.op("dve", lambda e: e.memset(ones_bf[:], 1.0), writes=["const"])
    kb.op("dve", lambda e: e.memset(onespad[:], 0.0), writes=["const"])
    kb.op("dve", lambda e: e.memset(onespad[:, 0, 0:64], 1.0), reads=["const"], writes=["const"])
    kb.op("dve", lambda e: e.memset(onespad[:, 1, 64:128], 1.0), reads=["const"], writes=["const"])
    kb.op("dve", lambda e: e.memset(Rb[:], 0.0), writes=["Vpad", "ksT", "gT"])
    for j in range(4):
        kb.op("dve", lambda e, j=j: e.tensor_copy(out=maskc[:, j, :], in_=cmask), reads=["const"], writes=["const"])
        ts(maskp[:, j, :], cmask, -1.0, 1.0, ALU.mult, ALU.add, ["const"], ["const"])
    for j in range(4):
        ts(maskp0[:, j, :], maskp[:, j, :], pcs[:, 16:17], None, ALU.mult, ALU.bypass, ["const"], ["const"])
    for l in range(depth):
        act(esink[:, l * 4:(l + 1) * 4], cp(l, 216, 4), AF.Exp, ["const"], ["const"])

    kb.barrier()
    CUT = int(os.environ.get("KDBG_CUT", "0"))
    STOPL = int(os.environ.get("KDBG_STOPL", "0"))

    def cut(k):
        if CUT == k:
            raise StopBuild()

    def body():
      for l in range(depth):
          if stop_after <= 0 and l == STOPL:
              break
          if phase != "C":
              kb.op("dve", lambda e: e.memset(S_run[:], 0.0), reads=["S_run"], writes=["S_run"])
              kb.op("dve", lambda e: e.memset(Ploc[:, 0, :], 1.0), reads=["Ploc"], writes=["Ploc"])
              for n in range(NT):
                  norm_tile(l, n, 0)
                  Wgk, wn_gk = W(l, "gkl")
                  glT = gla_glow(l, Wgk, wn_gk)
                  Wv0, wn0 = W(l, "gv0")
                  Wv1, wn1 = W(l, "gv1")
                  for blk in range(4):
                      b = n * 4 + blk
                      tb = slice(blk * 128, (blk + 1) * 128)
                      Lt = gla_L(l, glT, tb)
                      pk = PS.get()
                      for kc in range(8):
                          mm(pk.ap[:, 0:256], hT[:, kc, tb], Wgk[:, kc, 0:256], kc == 0, kc == 7, [wn_gk, "hT"], pk.nm)
                      prb = PS.get()
                      mm(prb.ap[:, 0:256], TriB, Lt.ap[:, 0:256], True, True, [Lt.nm, "const"], prb.nm)
                      erb = SF.get()
                      act(erb.ap[:, 0:256], prb.ap[:, 0:256], AF.Exp, [prb.nm], [erb.nm])
                      khat = SBF.get()
                      tt(khat.ap[:, 0:256], pk.ap[:, 0:256], erb.ap[:, 0:256], ALU.mult, [pk.nm, erb.nm], [khat.nm])
                      for p in range(2):
                          mm(PS_X[:, p * 2:p * 2 + 2], Lt.ap[:, p * 128:(p + 1) * 128], TriA[:, 126:128], True, True, [Lt.nm, "const"], "ps_x")
                      act(Ebt[:].rearrange("p q c -> p (q c)"), PS_X[:, 0:4], AF.Exp, ["ps_x"], ["Ebt"])
                      Vt = SBF.get()
                      v_tok(Wv0, wn0, Wv1, wn1, tb, Vt.ap, Vt.nm)
                      pd = PS.get()
                      for p in range(2):
                          mm(pd.ap[:, p * 256:(p + 1) * 256], khat.ap[:, p * 128:(p + 1) * 128], Vt.ap[:, p * 256:(p + 1) * 256], True, True, [khat.nm, Vt.nm], pd.nm)
                      kb.op("dve", lambda e, b=b: e.tensor_copy(out=Sloc[:, b, :, :], in_=S_run[:]), reads=["S_run"], writes=["Sloc"])
                      for p in range(2):
                          for hp in range(2):
                              pr = slice(hp * 64, (hp + 1) * 64)
                              stt(S_run[pr, p, :], S_run[pr, p, :], Ebt[pr, p, 1:2], pd.ap[pr, p * 256 + hp * 128:p * 256 + (hp + 1) * 128],
                                  ALU.mult, ALU.add, ["S_run", "Ebt", pd.nm], ["S_run"])
                      tt(Ploc[:, b + 1, :], Ploc[:, b, :], Ebt[:, :, 1], ALU.mult, ["Ploc", "Ebt"], ["Ploc"])
                  if n == NT - 1:
                      Wskv, wn_s = W(l, "skv")
                      tl = slice(384, 512)
                      pkt = PS.get()
                      for kc in range(8):
                          mm(pkt.ap[:, 0:128], Wskv[:, kc, 0:128], hT[:, kc, tl], kc == 0, kc == 7, [wn_s, "hT"], pkt.nm)
                      tsc = SF.get()
                      act(tsc.ap[:, 0:128], pkt.ap[:, 0:128], AF.Identity, [pkt.nm], [tsc.nm])
                      pvt = PS.get()
                      for kc in range(8):
                          mm(pvt.ap[:, 0:128], hT[:, kc, tl], Wskv[:, kc, 128:256], kc == 0, kc == 7, [wn_s, "hT"], pvt.nm)
                      act(tsc.ap[:, 128:256], pvt.ap[:, 0:128], AF.Identity, [pvt.nm, tsc.nm], [tsc.nm])
                      put = PS.get()
                      for gg in range(2):
                          Wpu, wn_p = W(l, "pu%d" % gg)
                          for g2 in range(2):
                              g = gg * 2 + g2
                              for kc in range(8):
                                  mm(put.ap[:, g * 16:(g + 1) * 16], Wpu[:, kc, g2 * 128:(g2 + 1) * 128], hT[:, kc, 496:512], kc == 0, kc == 7, [wn_p, "hT"], put.nm)
                      act(tsc.ap[:, 256:320], put.ap[:, 0:64], AF.Identity, [put.nm, tsc.nm], [tsc.nm])
                      kb.dma("sp", "cc", ccA_in[l][:, 264:584], tsc.ap[:, 0:320], reads=[tsc.nm], writes=["ccAi"])
              if stop_after <= 1 and l == STOPL:
                  break
              kb.dma("sp", "cc", ccA_in[l][:, 0:256], S_run[:].rearrange("p q v -> p (q v)"), reads=["S_run"], writes=["ccAi"])
              kb.op("dve", lambda e: e.memset(At8[:], 0.0), reads=["At8"], writes=["At8"])
              kb.op("dve", lambda e: e.tensor_copy(out=At8[:, 0:2], in_=Ploc[:, NB, :]), reads=["Ploc", "At8"], writes=["At8"])
              kb.dma("sp", "cc", ccA_in[l][:, 256:264], At8[:], reads=["At8"], writes=["ccAi"])
              if phase == "A":
                  break
              if phase is None:
                  kb.wait_all("pool", ["ccAi"])
                  kb.wait_all("pool", ["ccAo"])
                  kb.barrier_dma("pool")
                  ins = nc.gpsimd.collective_compute("AllGather", ALU.bypass, replica_groups=[list(range(NCORES))],
                                                     ins=[ccA_in[l].ap().opt()], outs=[ccA_out[l].ap().opt()])
                  sem = kb.dma_sem("ccx")
                  kb.dcnt["ccx"] += 1
                  ins.then_inc(sem, 1)
                  kb._record((sem, kb.dcnt["ccx"], "dma", True), [], ["ccAo"])
                  kb.wait_all("pool", ["ccAo"])
              gsrc = ccA_out[l].ap().rearrange("(r p) f -> p r f", p=128)
              zflat = zbuf[:].rearrange("p m t -> p (m t)")
              gFA = zflat[:, 0:8 * 264].rearrange("p (r f) -> p r f", r=8)
              kb.dma("sp", "ld", gFA, gsrc[:, :, 0:264], reads=["ccAo"], writes=["zbuf"])
              kb.op("dve", lambda e: e.memset(S_in[:], 0.0), reads=["S_in"], writes=["S_in"])
              for c2 in range(NCORES):
                  mcol = pcs[:, c2:c2 + 1]
                  Fm = SF.get()
                  ts(Fm.ap[:, 0:256], gFA[:, c2, 0:256], mcol, None, ALU.mult, ALU.bypass, ["zbuf", "const"], [Fm.nm])
                  ts(Am[:], gFA[:, c2, 256:258], -1.0, mcol, ALU.add, ALU.mult, ["zbuf", "const"], ["Am"])
                  ts(Am[:], Am[:], 1.0, None, ALU.add, ALU.bypass, ["Am"], ["Am"])
                  for p in range(2):
                      stt(S_in[:, p, :], S_in[:, p, :], Am[:, p:p + 1], Fm.ap[:, p * 128:(p + 1) * 128], ALU.mult, ALU.add, ["S_in", "Am", Fm.nm], ["S_in"])
              for b in range(NB):
                  for p in range(2):
                      stt(Sloc[:, b, p, :], S_in[:, p, :], Ploc[:, b, p:p + 1], Sloc[:, b, p, :], ALU.mult, ALU.add, ["S_in", "Ploc", "Sloc"], ["Sloc"])
              gTL = zflat[:, 0:8 * 320].rearrange("p (r f) -> p r f", r=8)
              kb.dma("sp", "ld", gTL, gsrc[:, :, 264:584], reads=["ccAo"], writes=["zbuf"])
              ts(halo[:], gTL[:, 0, :], pcs[:, 8:9], None, ALU.mult, ALU.bypass, ["zbuf", "const"], ["halo"])
              for c2 in range(1, NCORES):
                  stt(halo[:], gTL[:, c2, :], pcs[:, 8 + c2:9 + c2], halo[:], ALU.mult, ALU.add, ["zbuf", "const", "halo"], ["halo"])
              for g in range(2):
                  kb.op("dve", lambda e, g=g: e.tensor_copy(out=ksp[g * 64:(g + 1) * 64, g, 0:128], in_=halo[g * 64:(g + 1) * 64, 0:128]), reads=["halo"], writes=["ksT"])
              for g in range(2):
                  kb.op("dve", lambda e, g=g: e.tensor_copy(out=Vpad[:, 0, g, g * 64:(g + 1) * 64], in_=halo[:, 128 + g * 64:128 + (g + 1) * 64]), reads=["halo"], writes=["Vpad"])
              kb.op("dve", lambda e: e.tensor_copy(out=uh[:].rearrange("p g t -> p (g t)"), in_=halo[:, 256:320]), reads=["halo"], writes=["uh"])

              if stop_after <= 2 and l == STOPL:
                  break
              for n in range(NT):
                  cols = slice(n * N, (n + 1) * N)
                  norm_tile(l, n, 0)
                  Wgk, wn_gk = W(l, "gkl")
                  glT = gla_glow(l, Wgk, wn_gk)
                  Wv0, wn0 = W(l, "gv0")
                  Wv1, wn1 = W(l, "gv1")
                  for blk in range(4):
                      tb = slice(blk * 128, (blk + 1) * 128)
                      Lt = gla_L(l, glT, tb)
                      pb = PS.get()
                      for p in range(2):
                          mm(pb.ap[:, p * 128:(p + 1) * 128], Lt.ap[:, p * 128:(p + 1) * 128], TriA, True, True, [Lt.nm, "const"], pb.nm)
                      act(ebT[:, :, tb], pb.ap[:, 0:256].rearrange("p (q t) -> p q t", q=2), AF.Exp, [pb.nm], ["ebT"])
                      act(enbT[:, :, tb], pb.ap[:, 0:256].rearrange("p (q t) -> p q t", q=2), AF.Exp, [pb.nm], ["enbT"], scale=-1.0)
                      v_tok(Wv0, wn0, Wv1, wn1, tb, Vg[:, blk, :], "Vg")
                  for p in range(2):
                      pk = PS.get()
                      for kc in range(8):
                          mm(pk.ap, Wgk[:, kc, p * 128:(p + 1) * 128], hT[:, kc, :], kc == 0, kc == 7, [wn_gk, "hT"], pk.nm)
                      tt(kgT[:, p, :], pk.ap, enbT[:, p, :], ALU.mult, [pk.nm, "enbT"], ["kgT"])
                  Wq, wn_q = W(l, "gq")
                  for p in range(2):
                      pq = PS.get()
                      for kc in range(8):
                          mm(pq.ap, Wq[:, kc, p * 128:(p + 1) * 128], hT[:, kc, :], kc == 0, kc == 7, [wn_q, "hT"], pq.nm)
                      for hp in range(2):
                          pr = slice(hp * 64, (hp + 1) * 64)
                          stt(qgp[pr, 2 * p + hp, :], pq.ap[pr, :], 0.125, ebT[pr, p, :], ALU.mult, ALU.mult, [pq.nm, "ebT"], ["qgT"])
                  for hh in range(2):
                      Wr, wn_r = W(l, "gr%d" % hh)
                      for h2 in range(2):
                          h = hh * 2 + h2
                          pr_ = PS.get()
                          for kc in range(8):
                              mm(pr_.ap, Wr[:, kc, h2 * 128:(h2 + 1) * 128], hT[:, kc, :], kc == 0, kc == 7, [wn_r, "hT"], pr_.nm)
                          act(srT[:, h, :], pr_.ap, AF.Silu, [pr_.nm], ["srT"])
                  cut(1)
                  for blk in range(4):
                      b = n * 4 + blk
                      tb = slice(blk * 128, (blk + 1) * 128)
                      pa = PS.get()
                      for h in range(4):
                          p, hp = divmod(h, 2)
                          pr = slice(hp * 64, (hp + 1) * 64)
                          mm(pa.ap[:, h * 128:(h + 1) * 128], kgT[:, p, tb], qgp[:, h, tb], True, True, ["kgT", "qgT"], pa.nm)
                      at = SBF.get()
                      tt(at.ap, pa.ap, maskc[:].rearrange("p j q -> p (j q)"), ALU.mult, [pa.nm, "const"], [at.nm])
                      po = PS.get()
                      for h in range(4):
                          p, hp = divmod(h, 2)
                          pr = slice(hp * 64, (hp + 1) * 64)
                          mm(po.ap[:, h * 128:(h + 1) * 128], Vg[:, blk, h * 128:(h + 1) * 128], at.ap[:, h * 128:(h + 1) * 128], True, False, ["Vg", at.nm], po.nm)
                          mm(po.ap[:, h * 128:(h + 1) * 128], Sloc[:, b, p, :], qgp[:, h, tb], False, True, ["Sloc", "qgT"], po.nm)
                      sq = SBF.get()
                      act(sq.ap, po.ap, AF.Square, [po.nm], [sq.nm])
                      mm(PS_ST, ones_bf[:, :], sq.ap, True, True, [sq.nm, "const"], "ps_st")
                      r = rstd_from(PS_ST, "ps_st", N, 1.0 / 128.0)
                      t1 = SF.get()
                      tt(t1.ap[:, 0:N].rearrange("p (h t) -> p h t", h=4), r.ap[:, 0:N].rearrange("p (h t) -> p h t", h=4), srT[:, :, tb], ALU.mult, [r.nm, "srT"], [t1.nm])
                      for h in range(4):
                          stt(glao[:, h, tb], po.ap[:, h * 128:(h + 1) * 128], cp(l, 32 + h), t1.ap[:, h * 128:(h + 1) * 128], ALU.mult, ALU.mult, [po.nm, t1.nm, "const"], ["glao"])
                  cut(2)
                  for jj in range(2):
                      Wsq, wn_sq = W(l, "sq%d" % jj)
                      for j2 in range(2):
                          j = jj * 2 + j2
                          pq = PS.get()
                          for kc in range(8):
                              mm(pq.ap, Wsq[:, kc, j2 * 128:(j2 + 1) * 128], hT[:, kc, :], kc == 0, kc == 7, [wn_sq, "hT"], pq.nm)
                          act(qsT[:, j, :], pq.ap, AF.Identity, [pq.nm], ["qsT"])
                  Wskv, wn_s = W(l, "skv")
                  pk = PS.get()
                  for kc in range(8):
                      mm(pk.ap, Wskv[:, kc, 0:128], hT[:, kc, :], kc == 0, kc == 7, [wn_s, "hT"], pk.nm)
                  for g in range(2):
                      act(ksp[g * 64:(g + 1) * 64, g, 128:640], pk.ap[g * 64:(g + 1) * 64, :], AF.Identity, [pk.nm], ["ksT"])
                  for blk in range(4):
                      tb = slice(blk * 128, (blk + 1) * 128)
                      pv = PS.get()
                      for kc in range(8):
                          mm(pv.ap[:, 0:128], hT[:, kc, tb], Wskv[:, kc, 128:256], kc == 0, kc == 7, [wn_s, "hT"], pv.nm)
                      for g in range(2):
                          act(Vpad[:, 1 + blk, g, g * 64:(g + 1) * 64], pv.ap[:, g * 64:(g + 1) * 64], AF.Identity, [pv.nm], ["Vpad"])
                  for blk in range(4):
                      b = n * 4 + blk
                      tb = slice(blk * 128, (blk + 1) * 128)
                      probs = []
                      for g in range(2):
                          pr = slice(g * 64, (g + 1) * 64)
                          for which in range(2):
                              kcols = slice(128 + blk * 128, 256 + blk * 128) if which == 0 else slice(blk * 128, 128 + blk * 128)
                              psc = PS.get()
                              mm(psc.ap.rearrange("p (j q) -> p j q", j=4), ksp[:, g, kcols], qsT[:, :, tb], True, True, ["ksT", "qsT"], psc.nm)
                              pe_ = SBF.get()
                              act(pe_.ap, psc.ap, AF.Exp, [psc.nm], [pe_.nm], scale=0.125)
                              msk = maskc if which == 0 else (maskp0 if b == 0 else maskp)
                              tt(pe_.ap, pe_.ap, msk[:].rearrange("p j q -> p (j q)"), ALU.mult, [pe_.nm, "const"], [pe_.nm])
                              probs.append((pe_, g, which))
                      po = PS.get()
                      pdn = PS.get()
                      for i, (pe_, g, which) in enumerate(probs):
                          vb = 1 + blk if which == 0 else blk
                          mm(po.ap, Vpad[:, vb, g, :], pe_.ap, i == 0, i == 3, ["Vpad", pe_.nm], po.nm)
                      for i, (pe_, g, which) in enumerate(probs):
                          mm(pdn.ap, onespad[:, g, :], pe_.ap, i == 0, i == 3, ["const", pe_.nm], pdn.nm)
                      den = SF.get()
                      for j in range(4):
                          ts(den.ap[:, j * 128:(j + 1) * 128], pdn.ap[:, j * 128:(j + 1) * 128], esink[:, l * 4 + j:l * 4 + j + 1], None, ALU.add, ALU.bypass, [pdn.nm, "const", den.nm], [den.nm])
                      kb.op("dve", lambda e, den=den: e.reciprocal(out=den.ap[:, 0:N], in_=den.ap[:, 0:N]), reads=[den.nm], writes=[den.nm])
                      tt(swao[:, :, tb], po.ap.rearrange("p (j q) -> p j q", j=4), den.ap[:, 0:N].rearrange("p (j q) -> p j q", j=4), ALU.mult, [po.nm, den.nm], ["swao"])
                  for g in range(2):
                      kb.op("dve", lambda e, g=g: e.tensor_copy(out=ksp[g * 64:(g + 1) * 64, g, 0:128], in_=ksp[g * 64:(g + 1) * 64, g, 512:640]), reads=["ksT"], writes=["ksT"])
                  for g in range(2):
                      kb.op("dve", lambda e, g=g: e.tensor_copy(out=Vpad[:, 0, g, g * 64:(g + 1) * 64], in_=Vpad[:, 4, g, g * 64:(g + 1) * 64]), reads=["Vpad"], writes=["Vpad"])
                  cut(3)
                  Wpw, wn_pw = W(l, "pw")
                  for gg in range(2):
                      Wpu, wn_p = W(l, "pu%d" % gg)
                      for g2 in range(2):
                          g = gg * 2 + g2
                          w = 2 ** (g + 1)
                          pu_ = PS.get()
                          for kc in range(8):
                              mm(pu_.ap, Wpu[:, kc, g2 * 128:(g2 + 1) * 128], hT[:, kc, :], kc == 0, kc == 7, [wn_p, "hT"], pu_.nm)
                          ub = SF.get()
                          act(ub.ap[:, 16:528], pu_.ap, AF.Identity, [pu_.nm], [ub.nm])
                          kb.op("dve", lambda e, ub=ub, g=g: e.tensor_copy(out=ub.ap[:, 0:16], in_=uh[:, g, :]), reads=["uh", ub.nm], writes=[ub.nm])
                          kb.op("dve", lambda e, ub=ub, g=g: e.tensor_copy(out=uh[:, g, :], in_=ub.ap[:, 512:528]), reads=[ub.nm, "uh"], writes=["uh"])
                          cur = ub
                          lo, sh = 0, 1
                          while sh < w:
                              nx = SF.get()
                              lo2 = lo + sh
                              tt(nx.ap[:, lo2:528], cur.ap[:, lo2:528], cur.ap[:, lo2 - sh:528 - sh], ALU.add, [cur.nm], [nx.nm])
                              cur, lo, sh = nx, lo2, sh * 2
                          dT = SBF.get()
                          stt(dT.ap, cur.ap[:, 16:528], 1.0 / w, ub.ap[:, 16:528], ALU.mult, ALU.subtract, [cur.nm, ub.nm], [dT.nm])
                          if n == 0:
                              tmp = SF.get()
                              tt(tmp.ap[:, 0:16], cur.ap[:, 16:32], pcs[:, 32 + g * 16:48 + g * 16], ALU.mult, [cur.nm, "const"], [tmp.nm])
                              tt(dT.ap[:, 0:16], tmp.ap[:, 0:16], ub.ap[:, 16:32], ALU.subtract, [tmp.nm, ub.nm, dT.nm], [dT.nm])
                          pp = PS.get()
                          mm(pp.ap, Wpw[:, g, :], dT.ap, True, True, [wn_pw, dT.nm], pp.nm)
                          ts(pcT[:, g, :], pp.ap, cp(l, 36 + g), None, ALU.mult, ALU.bypass, [pp.nm, "const"], ["pcT"])
                  cut(4)
                  for m in range(8):
                      Wgt, wn_g = W(l, "gt%d" % m)
                      Wbr, wn_b = W(l, "br%d" % m)
                      sigs = []
                      for bi in range(3):
                          pg = PS.get()
                          for kc in range(8):
                              mm(pg.ap, Wgt[:, kc, bi * 128:(bi + 1) * 128], hT[:, kc, :], kc == 0, kc == 7, [wn_g, "hT"], pg.nm)
                          sg = SBF.get()
                          act(sg.ap, pg.ap, AF.Sigmoid, [pg.nm], [sg.nm])
                          sigs.append(sg)
                      t0 = None
                      for bi, (src, sname) in enumerate(((glao, "glao"), (swao, "swao"), (pcT, "pcT"))):
                          py = PS.get()
                          for c in range(4):
                              mm(py.ap, Wbr[:, bi * 4 + c, :], src[:, c, :], c == 0, c == 3, [wn_b, sname], py.nm)
                          if bi == 0:
                              t0 = SF.get()
                              tt(t0.ap[:, 0:N], py.ap, sigs[0].ap, ALU.mult, [py.nm, sigs[0].nm], [t0.nm])
                          else:
                              t1 = SF.get()
                              tt(t1.ap[:, 0:N], py.ap, sigs[bi].ap, ALU.mult, [py.nm, sigs[bi].nm], [t1.nm])
                              if bi == 1:
                                  tt(t0.ap[:, 0:N], t0.ap[:, 0:N], t1.ap[:, 0:N], ALU.add, [t0.nm, t1.nm], [t0.nm])
                              else:
                                  tt(mrg[:, m, :], t0.ap[:, 0:N], t1.ap[:, 0:N], ALU.add, [t0.nm, t1.nm], ["mrg"])
                  cut(5)
                  for m2 in range(8):
                      Wo, wn_o = W(l, "wo%d" % m2)
                      pz = PS.get()
                      for m in range(8):
                          mm(pz.ap, Wo[:, m, :], mrg[:, m, :], m == 0, m == 7, [wn_o, "mrg"], pz.nm)
                      act(zbuf[:, m2, :], pz.ap, AF.Identity, [pz.nm], ["zbuf"])
                      sq = SBF.get()
                      act(sq.ap, pz.ap, AF.Square, [pz.nm], [sq.nm])
                      mm(PS_ST, ones_bf[:, :], sq.ap, m2 == 0, m2 == 7, [sq.nm, "const"], "ps_st")
                  post_norm_residual(l, n, 8)

          if stop_after <= 3 and l == STOPL:
              break
          if phase == "B":
              break
          kb.barrier()
          if phase is None:
              kb.op("dve", lambda e: e.tensor_copy(out=xt2[:], in_=xT[:, :, TC - 2:TC]), reads=["xT", "xt2"], writes=["xt2"])
              kb.dma("sp", "cc", ccB_in[l][:, :], xt2[:].rearrange("p k t -> p (k t)"), reads=["xt2"], writes=["ccBi"])
              kb.wait_all("pool", ["ccBi"])
              kb.wait_all("pool", ["ccBo"])
              kb.barrier_dma("pool")
              ins = nc.gpsimd.collective_compute("AllGather", ALU.bypass, replica_groups=[list(range(NCORES))],
                                                 ins=[ccB_in[l].ap().opt()], outs=[ccB_out[l].ap().opt()])
              sem = kb.dma_sem("ccx")
              kb.dcnt["ccx"] += 1
              ins.then_inc(sem, 1)
              kb._record((sem, kb.dcnt["ccx"], "dma", True), [], ["ccBo"])
              kb.wait_all("pool", ["ccBo"])
          gB = zbuf[:].rearrange("p m t -> p (m t)")[:, 0:8 * XB].rearrange("p (r f) -> p r f", r=8)
          kb.dma("sp", "ld", gB, ccB_out[l].ap().rearrange("(r p) f -> p r f", p=128), reads=["ccBo"], writes=["zbuf"])
          ts(xh[:], gB[:, 0, :], pcs[:, 8:9], None, ALU.mult, ALU.bypass, ["zbuf", "const"], ["xh"])
          for c2 in range(1, NCORES):
              stt(xh[:], gB[:, c2, :], pcs[:, 8 + c2:9 + c2], xh[:], ALU.mult, ALU.add, ["zbuf", "const", "xh"], ["xh"])
          xh3 = xh[:].rearrange("p (k t) -> p k t", k=8)
          sqh = SBF.get()
          act(sqh.ap[:, 0:16], xh[:], AF.Square, ["xh"], [sqh.nm])
          for kc in range(8):
              mm(PS_X[:, 8:10], ones_bf[:, :], sqh.ap[:, kc * 2:kc * 2 + 2], kc == 0, kc == 7, [sqh.nm, "const"], "ps_x")
          rh = rstd_from(PS_X[:, 8:10], "ps_x", 2, 1.0 / D)
          for kc in range(8):
              stt(h2h[:, kc, :], xh3[:, kc, :], cp(l, 16 + kc), rh.ap[:, 0:2], ALU.mult, ALU.mult, ["xh", rh.nm, "const"], ["h2h"])

          if stop_after <= 4 and l == STOPL:
              break
          for n in range(NT):
              norm_tile(l, n, 16)
              for j in range(NFC):
                  Wu, wn_u = W(l, "up%d" % j)
                  gact = None
                  for half in range(2):
                      cidx = j + NFC * half
                      pu_ = PS.get()
                      for kc in range(8):
                          mm(pu_.ap, Wu[:, kc, half * 128:(half + 1) * 128], hT[:, kc, :], kc == 0, kc == 7, [wn_u, "hT"], pu_.nm)
                      U = SF.get()
                      act(U.ap[:, 2:514], pu_.ap, AF.Identity, [pu_.nm], [U.nm])
                      if n == 0:
                          for kc in range(8):
                              mm(PS_X[:, 16:18], Wu[:, kc, half * 128:(half + 1) * 128], h2h[:, kc, :], kc == 0, kc == 7, [wn_u, "h2h"], "ps_x")
                          act(U.ap[:, 0:2], PS_X[:, 16:18], AF.Identity, ["ps_x", U.nm], [U.nm])
                      else:
                          kb.op("dve", lambda e, U=U, cidx=cidx: e.tensor_copy(out=U.ap[:, 0:2], in_=uph[:, cidx, :]), reads=["uph", U.nm], writes=[U.nm])
                      kb.op("dve", lambda e, U=U, cidx=cidx: e.tensor_copy(out=uph[:, cidx, :], in_=U.ap[:, 512:514]), reads=[U.nm, "uph"], writes=["uph"])
                      acc = SF.get()
                      ts(acc.ap[:, 0:N], U.ap[:, 2:514], cp(l, 40 + 2 * 44 + cidx), cp(l, 172 + cidx), ALU.mult, ALU.add, [U.nm, "const"], [acc.nm])
                      stt(acc.ap[:, 0:N], U.ap[:, 1:513], cp(l, 40 + 1 * 44 + cidx), acc.ap[:, 0:N], ALU.mult, ALU.add, [U.nm, acc.nm, "const"], [acc.nm])
                      stt(acc.ap[:, 0:N], U.ap[:, 0:512], cp(l, 40 + cidx), acc.ap[:, 0:N], ALU.mult, ALU.add, [U.nm, acc.nm, "const"], [acc.nm])
                      if half == 0:
                          gact = SF.get()
                          act(gact.ap[:, 0:N], acc.ap[:, 0:N], AF.Gelu_apprx_tanh, [acc.nm], [gact.nm])
                      else:
                          tt(gT[:, j, :], gact.ap[:, 0:N], acc.ap[:, 0:N], ALU.mult, [gact.nm, acc.nm], ["gT"])
              for m in range(8):
                  Wd, wn_d = W(l, "dn%d" % m)
                  pz = PS.get()
                  for fc in range(NFC):
                      mm(pz.ap, Wd[:, fc, :], gT[:, fc, :], fc == 0, fc == NFC - 1, [wn_d, "gT"], pz.nm)
                  act(zbuf[:, m, :], pz.ap, AF.Identity, [pz.nm], ["zbuf"])
                  sq = SBF.get()
                  act(sq.ap, pz.ap, AF.Square, [pz.nm], [sq.nm])
                  mm(PS_ST, ones_bf[:, :], sq.ap, m == 0, m == 7, [sq.nm, "const"], "ps_st")
              post_norm_residual(l, n, 24)
          kb.barrier()

    try:
        body()
    except StopBuild:
        pass
    if phase == "A":
        kb.wait_all("sp", ["ccAi"])
    else:
        kb.dma("sp", "st", yT_d, xT[:], reads=["xT"], writes=["yT"])
        kb.wait_all("sp", ["yT"])
    kb.barrier()
    for cm in reversed(ctxs):
        cm.__exit__(None, None, None)
    kb.close()
    return nc, kb


_CACHE = {}
FUSED = True


def _launch(inp, depth, stop_after, xTs, phase=None, extra=None, outs=("yT",)):
    wpack = np.stack([_pack_weights(inp, l) for l in range(depth)], axis=0)
    if os.environ.get("KDBG_SMALLW"):
        wpack = np.ascontiguousarray(wpack[:, :128 * SLOT])
    cpack = _pack_small(inp)
    wgb = np.zeros((17, DEPTH * 256), np.float32)
    for l in range(DEPTH):
        wgb[0:16, l * 256:(l + 1) * 256] = inp["gla_w_gate_up"][l]
        wgb[16, l * 256:(l + 1) * 256] = inp["gla_b_gate"][l]
    in_maps = []
    for c in range(NCORES):
        cst, pc = _consts(c)
        pad = float(c) if phase is None else 0.0
        wpc = np.concatenate([wpack, np.full((1, wpack.shape[1]), pad, np.float32)], axis=0)
        m = {"xT": xTs[c], "wpack": wpc, "cpack": cpack, "wgb": wgb, "cst": cst, "pcst": pc}
        if extra:
            m.update(extra)
        in_maps.append(m)
    key = (depth, stop_after, phase)
    if key not in _CACHE:
        _CACHE[key] = build(depth, stop_after, phase)[0]
    res = run_bass_kernel_spmd(_CACHE[key], in_maps, core_ids=list(range(NCORES)))
    if os.environ.get("KDBG_VERBOSE"):
        print("launch done depth", depth, phase, flush=True)
    return [[np.asarray(res.results[c][o]) for c in range(NCORES)] for o in outs]


def kernel(**inp):
    inp = {k: np.asarray(v) for k, v in inp.items()}
    depth = int(inp.pop("_depth", DEPTH)) if "_depth" in inp else DEPTH
    stop_after = int(inp.pop("_stop", 99)) if "_stop" in inp else 99
    fused = bool(inp.pop("_fused", FUSED)) if "_fused" in inp else FUSED
    x = inp["x"][0]
    xTs = []
    for c in range(NCORES):
        xs = x[c * TC:(c + 1) * TC]
        xTs.append(np.ascontiguousarray(xs.T.reshape(8, 128, TC).transpose(1, 0, 2)))
    if fused:
        xTs = _launch(inp, depth, stop_after, xTs)[0]
    else:
        for l in range(depth):
            inp_l = {k: (v if k == "x" else np.repeat(v[l:l + 1], DEPTH, axis=0)) for k, v in inp.items()}
            payA = _launch(inp_l, 1, 99, xTs, phase="A", outs=("payA",))[0]
            gA = np.ascontiguousarray(np.concatenate(payA, axis=0))
            xTs = _launch(inp_l, 1, 99, xTs, phase="B", extra={"gA": gA})[0]
            gB = np.ascontiguousarray(np.concatenate([t[:, :, TC - 2:TC].reshape(128, XB) for t in xTs], axis=0))
            xTs = _launch(inp_l, 1, 99, xTs, phase="C", extra={"gB": gB})[0]
    out = np.empty((1, T_ALL, D), np.float32)
    for c in range(NCORES):
        out[0, c * TC:(c + 1) * TC] = xTs[c].transpose(1, 0, 2).reshape(D, TC).T
    return out
```

```python
import os
import numpy as np
import concourse.bass as bass
import concourse.mybir as mybir
from concourse.bass_utils import run_bass_kernel_spmd

F32 = mybir.dt.float32
BF16 = mybir.dt.bfloat16
AF = mybir.ActivationFunctionType
ALU = mybir.AluOpType

NCORES = 8
T_ALL = 16384
TC = 2048
D = 1024
DEPTH = 4
N = 512
NT = TC // N
NB = TC // 128
DFF = 2816
NFC = DFF // 128
EPS = 1e-6
O_GQ, O_GK, O_GV, O_GLOW, O_GR, O_SQ, O_SK, O_SV, O_PU, O_GATE = 0, 256, 512, 1024, 1040, 1552, 2064, 2192, 2320, 2832
CP = 220
XA = 584
XB = 16
SLOT = 3072
NSLOT = 4


class KB:
    def __init__(self, nc):
        self.nc = nc
        self.engs = {"pe": nc.tensor, "act": nc.scalar, "dve": nc.vector, "pool": nc.gpsimd, "sp": nc.sync}
        self.esem, self.ecnt, self._ctx = {}, {}, []
        for n in ("pe", "act", "dve", "pool"):
            cm = nc.semaphore("es_" + n)
            self.esem[n] = cm.__enter__()
            self._ctx.append(cm)
            self.ecnt[n] = 0
        self.known = {n: {} for n in self.engs}
        self.lastw, self.readers = {}, {}
        self.dsem, self.dcnt = {}, {}
        self.nwaits = 0
        self.nins = 0

    def dma_sem(self, key):
        if key not in self.dsem:
            cm = self.nc.semaphore("ds_" + str(key))
            self.dsem[key] = cm.__enter__()
            self._ctx.append(cm)
            self.dcnt[key] = 0
        return self.dsem[key]

    def close(self):
        for cm in reversed(self._ctx):
            cm.__exit__(None, None, None)

    def _wait(self, en, sem, val):
        k = self.known[en]
        key = sem.name
        if k.get(key, 0) >= val:
            return
        k[key] = val
        self.engs[en].wait_ge(sem, val)
        self.nwaits += 1

    def _deps(self, en, reads, writes):
        need = []
        for b in reads:
            w = self.lastw.get(b)
            if w is not None and not (en == "pe" and w[2] == "pe"):
                need.append(w)
        for b in writes:
            w = self.lastw.get(b)
            if w is not None and not (en == "pe" and w[2] == "pe"):
                need.append(w)
            for r in self.readers.get(b, ()):
                if not (en == "pe" and r[2] == "pe"):
                    need.append(r)
        for ev in need:
            self._wait(en, ev[0], ev[1])

    def _record(self, ev, reads, writes):
        for b in writes:
            self.lastw[b] = ev
            self.readers[b] = []
        for b in reads:
            lst = self.readers.setdefault(b, [])
            for i, r in enumerate(lst):
                if r[0] is ev[0]:
                    lst[i] = ev
                    break
            else:
                lst.append(ev)

    def op(self, en, fn, reads=(), writes=()):
        reads = [getattr(r, "nm", r) for r in reads]
        writes = [getattr(r, "nm", r) for r in writes]
        self._deps(en, reads, writes)
        ins = fn(self.engs[en])
        self.ecnt[en] += 1
        self.nins += 1
        ins.then_inc(self.esem[en], 1)
        self._record((self.esem[en], self.ecnt[en], en, False), reads, writes)
        return ins

    def dma(self, q, key, out, in_, reads=(), writes=()):
        reads = [getattr(r, "nm", r) for r in reads]
        writes = [getattr(r, "nm", r) for r in writes]
        sem = self.dma_sem(key)
        self._deps(q, reads, writes)
        ins = self.engs[q].dma_start(out=out, in_=in_)
        self.dcnt[key] += 16
        self.nins += 1
        ins.then_inc(sem, 16)
        self._record((sem, self.dcnt[key], "dma", True), reads, writes)
        return ins

    def wait_all(self, en, bufs):
        for b in bufs:
            w = self.lastw.get(b)
            if w is not None:
                self._wait(en, w[0], w[1])

    def barrier_dma(self, en):
        for key, sem in self.dsem.items():
            if self.dcnt[key] > 0:
                self._wait(en, sem, self.dcnt[key])

    def barrier(self):
        for en in ("pe", "act", "dve", "pool", "sp"):
            for n2 in ("pe", "act", "dve", "pool"):
                if n2 != en and self.ecnt[n2] > 0:
                    self._wait(en, self.esem[n2], self.ecnt[n2])
            for key, sem in self.dsem.items():
                if self.dcnt[key] > 0:
                    self._wait(en, sem, self.dcnt[key])


class StopBuild(Exception):
    pass


class Rot:
    def __init__(self, name, aps):
        self.name, self.aps, self.i = name, aps, 0
        self.gen = [0] * len(aps)

    def get(self):
        i = self.i % len(self.aps)
        self.i += 1
        self.gen[i] += 1
        return RB(self, i, self.gen[i])


class RB:
    def __init__(self, rot, i, gen):
        self.rot, self.i, self.gen = rot, i, gen
        self.ap = rot.aps[i]
        self.n = "%s%d" % (rot.name, i)

    @property
    def nm(self):
        assert self.rot.gen[self.i] == self.gen, "stale rotating buffer %s" % self.n
        return self.n


def _wspec():
    s = {}
    s["gkl"] = (8, 272)
    s["gq"] = (8, 256)
    s["gv0"] = (8, 256); s["gv1"] = (8, 256)
    s["gr0"] = (8, 256); s["gr1"] = (8, 256)
    s["sq0"] = (8, 256); s["sq1"] = (8, 256)
    s["skv"] = (8, 256)
    s["pu0"] = (8, 256); s["pu1"] = (8, 256)
    s["pw"] = (4, 128)
    for m in range(8):
        s["gt%d" % m] = (8, 384)
        s["br%d" % m] = (12, 128)
        s["wo%d" % m] = (8, 128)
        s["dn%d" % m] = (NFC, 128)
    for j in range(NFC):
        s["up%d" % j] = (8, 256)
    off, o = {}, 0
    for k, (nk, wc) in s.items():
        off[k] = o
        o += 128 * nk * wc
    return s, off, o


WSPEC, WOFF, WTOT = _wspec()


def _img(w):
    K, wc = w.shape
    nk = K // 128
    return np.ascontiguousarray(w.reshape(nk, 128, wc).transpose(1, 0, 2)).reshape(128, nk * wc)


def _pack_weights(inp, l):
    w_in = inp["w_in"][l]
    out = np.empty((WTOT,), np.float32)

    def put(name, mat):
        nk, wc = WSPEC[name]
        assert mat.shape == (nk * 128, wc), (name, mat.shape)
        out[WOFF[name]:WOFF[name] + 128 * nk * wc] = _img(mat).reshape(-1)

    put("gkl", np.concatenate([w_in[:, O_GK:O_GK + 256], w_in[:, O_GLOW:O_GLOW + 16]], axis=1))
    put("gq", w_in[:, O_GQ:O_GQ + 256])
    put("gv0", w_in[:, O_GV:O_GV + 256]); put("gv1", w_in[:, O_GV + 256:O_GV + 512])
    put("gr0", w_in[:, O_GR:O_GR + 256]); put("gr1", w_in[:, O_GR + 256:O_GR + 512])
    sqc = []
    for j in range(4):
        for g in range(2):
            h = g * 4 + j
            sqc.append(w_in[:, O_SQ + h * 64:O_SQ + (h + 1) * 64])
    sqc = np.concatenate(sqc, axis=1)
    put("sq0", sqc[:, :256]); put("sq1", sqc[:, 256:])
    put("skv", w_in[:, O_SK:O_SK + 256])
    put("pu0", w_in[:, O_PU:O_PU + 256]); put("pu1", w_in[:, O_PU + 256:O_PU + 512])
    pw = inp["pool_w"][l]
    put("pw", pw.reshape(4 * 128, 128))
    wg_, ws_, wp_ = inp["w_branch_gla"][l], inp["w_branch_swa"][l], inp["w_branch_pool"][l]
    perm = np.array([g * 256 + j * 64 + d for j in range(4) for g in range(2) for d in range(64)])
    ws_p = ws_[perm]
    wo_ = inp["w_out"][l]
    wup, wdn = inp["ffn_w_up"][l], inp["ffn_w_down"][l]
    for m in range(8):
        cs = slice(m * 128, (m + 1) * 128)
        put("gt%d" % m, np.concatenate([w_in[:, O_GATE + b * 1024 + m * 128:O_GATE + b * 1024 + (m + 1) * 128] for b in range(3)], axis=1))
        put("br%d" % m, np.concatenate([wg_[:, cs], ws_p[:, cs], wp_[:, cs]], axis=0))
        put("wo%d" % m, wo_[:, cs])
        put("dn%d" % m, wdn[:, cs])
    for j in range(NFC):
        put("up%d" % j, np.concatenate([wup[:, j * 128:(j + 1) * 128], wup[:, DFF + j * 128:DFF + (j + 1) * 128]], axis=1))
    return out


def _pack_small(inp):
    cp = np.zeros((128, DEPTH * CP), np.float32)
    for l in range(DEPTH):
        b = l * CP
        for i, k in enumerate(["norm_mix_pre", "norm_mix_post", "norm_ffn_pre", "norm_ffn_post"]):
            cp[:, b + i * 8:b + (i + 1) * 8] = inp[k][l].reshape(8, 128).T
        cp[:, b + 32:b + 36] = inp["gla_norm"][l].reshape(4, 128).T
        cp[:, b + 36:b + 40] = inp["pool_scale"][l].reshape(4, 128).T
        cw = inp["ffn_conv_w"][l]
        for i in range(3):
            cp[:, b + 40 + i * 44:b + 40 + (i + 1) * 44] = cw[i].reshape(44, 128).T
        cp[:, b + 172:b + 216] = inp["ffn_conv_b"][l].reshape(44, 128).T
        sk = inp["swa_sinks"][l]
        cp[0:64, b + 216:b + 220] = sk[0:4][None, :]
        cp[64:128, b + 216:b + 220] = sk[4:8][None, :]
    return cp


def _consts(core):
    j = np.arange(128)[:, None]
    i = np.arange(128)[None, :]
    cst = np.zeros((128, 384), np.float32)
    cst[:, 0:128] = np.where(j <= i, -1.0 / 16.0, 0.0)
    cst[:, 128:256] = np.where(j > i, -1.0 / 16.0, 0.0)
    cst[:, 256:384] = np.where(j <= i, 1.0, 0.0)
    pc = np.zeros((128, 96), np.float32)
    for c2 in range(8):
        pc[:, c2] = 1.0 if c2 < core else 0.0
        pc[:, 8 + c2] = 1.0 if c2 == core - 1 else 0.0
    pc[:, 16] = 1.0 if core > 0 else 0.0
    for g in range(4):
        w = 2 ** (g + 1)
        for t in range(16):
            pos = core * TC + t + 1
            pc[:, 32 + g * 16 + t] = 1.0 / min(pos, w)
    return cst, pc


def build(depth=DEPTH, stop_after=99, phase=None):
    nc = bass.Bass("TRN2", target_bir_lowering=False)
    xT_d = nc.dram_tensor("xT", [128, 8, TC], F32, kind="ExternalInput").ap()
    SMALLW = bool(os.environ.get("KDBG_SMALLW"))
    wp_d = nc.dram_tensor("wpack", [depth + 1, 128 * SLOT if SMALLW else WTOT], F32, kind="ExternalInput").ap()
    cp_d = nc.dram_tensor("cpack", [128, DEPTH * CP], F32, kind="ExternalInput").ap()
    wgb_d = nc.dram_tensor("wgb", [17, DEPTH * 256], F32, kind="ExternalInput").ap()
    cst_d = nc.dram_tensor("cst", [128, 384], F32, kind="ExternalInput").ap()
    pc_d = nc.dram_tensor("pcst", [128, 96], F32, kind="ExternalInput").ap()
    yT_d = nc.dram_tensor("yT", [128, 8, TC], F32, kind="ExternalOutput").ap() if phase != "A" else None
    if phase == "A":
        ccA_in = [nc.dram_tensor("payA", [128, XA], F32, kind="ExternalOutput")]
    else:
        ccA_in = [nc.dram_tensor("ccAi%d" % l, [128, XA], F32) for l in range(depth)]
    if phase == "B":
        ccA_out = [nc.dram_tensor("gA", [NCORES * 128, XA], F32, kind="ExternalInput")]
    else:
        ccA_out = [nc.dram_tensor("ccAo%d" % l, [NCORES * 128, XA], F32) for l in range(depth)]
    ccB_in = [nc.dram_tensor("ccBi%d" % l, [128, XB], F32) for l in range(depth)]
    if phase == "C":
        ccB_out = [nc.dram_tensor("gB", [NCORES * 128, XB], F32, kind="ExternalInput")]
    else:
        ccB_out = [nc.dram_tensor("ccBo%d" % l, [NCORES * 128, XB], F32) for l in range(depth)]
    dbg_outs = {}

    kb = KB(nc)
    ctxs = []

    def sb(name, shape, dt):
        cm = nc.sbuf_tensor(name, shape, dt)
        t = cm.__enter__()
        ctxs.append(cm)
        return t

    xT = sb("xTs", [128, 8, TC], F32)
    wsl = sb("wsl", [128, NSLOT, SLOT], BF16)
    cpk = sb("cpk", [128, DEPTH * CP], F32)
    wgb = sb("wgbs", [17, DEPTH * 256], F32)
    cst = sb("csts", [128, 384], F32)
    pcs = sb("pcs", [128, 96], F32)
    maskc = sb("maskc", [128, 4, 128], BF16)
    maskp = sb("maskp", [128, 4, 128], BF16)
    maskp0 = sb("maskp0", [128, 4, 128], BF16)
    ones_bf = sb("ones_bf", [128, 128], BF16)
    onespad = sb("onespad", [128, 2, 128], BF16)
    esink = sb("esink", [128, DEPTH * 4], F32)
    hT = sb("hT", [128, 8, N], BF16)
    zbuf = sb("zbuf", [128, 8, N], F32)
    S_run = sb("S_run", [128, 2, 128], F32)
    S_in = sb("S_in", [128, 2, 128], F32)
    Ploc = sb("Ploc", [128, NB + 1, 2], F32)
    Ebt = sb("Ebt", [128, 2, 2], F32)
    Am = sb("Am", [128, 2], F32)
    At8 = sb("At8", [128, 8], F32)
    halo = sb("halo", [128, 320], F32)
    uh = sb("uh", [128, 4, 16], F32)
    uph = sb("uph", [128, 44, 2], F32)
    xh = sb("xh", [128, 16], F32)
    glTt = sb("glTt", [32, N], F32)
    xt2 = sb("xt2", [128, 8, 2], F32)
    RB_EL = 28304
    Rb = sb("Rb", [128, RB_EL], BF16)
    o = [0]

    def carve(n):
        a = o[0]
        o[0] += n
        return a

    o[0] = 0
    a_Sloc = carve(NB * 256); a_eb = carve(1024); a_enb = carve(1024); a_Vg = carve(2048)
    a_kg = carve(1024); a_sr = carve(2048); a_glao = carve(2048)
    a_qs = carve(2048); a_swao = carve(2048); a_pc = carve(2048); a_mrg = carve(4096)
    a_qg = carve(2048); a_ks = carve(1280); a_vp = carve(1280); a_h2h = carve(16)
    assert o[0] <= RB_EL, o[0]
    assert a_qg >= NFC * N
    Sloc = Rb[:, a_Sloc:a_Sloc + NB * 256].rearrange("p (b q v) -> p b q v", b=NB, q=2)
    ebT = Rb[:, a_eb:a_eb + 1024].rearrange("p (q t) -> p q t", q=2)
    enbT = Rb[:, a_enb:a_enb + 1024].rearrange("p (q t) -> p q t", q=2)
    Vg = Rb[:, a_Vg:a_Vg + 2048].rearrange("p (b v) -> p b v", b=4)
    kgT = Rb[:, a_kg:a_kg + 1024].rearrange("p (q t) -> p q t", q=2)
    qgp = Rb[:, a_qg:a_qg + 2048].rearrange("p (h t) -> p h t", h=4)
    srT = Rb[:, a_sr:a_sr + 2048].rearrange("p (h t) -> p h t", h=4)
    glao = Rb[:, a_glao:a_glao + 2048].rearrange("p (h t) -> p h t", h=4)
    qsT = Rb[:, a_qs:a_qs + 2048].rearrange("p (j t) -> p j t", j=4)
    swao = Rb[:, a_swao:a_swao + 2048].rearrange("p (j t) -> p j t", j=4)
    pcT = Rb[:, a_pc:a_pc + 2048].rearrange("p (g t) -> p g t", g=4)
    mrg = Rb[:, a_mrg:a_mrg + 4096].rearrange("p (m t) -> p m t", m=8)
    ksp = Rb[:, a_ks:a_ks + 1280].rearrange("p (g t) -> p g t", g=2)
    Vpad = Rb[:, a_vp:a_vp + 1280].rearrange("p (b g d) -> p b g d", b=5, g=2)
    h2h = Rb[:, a_h2h:a_h2h + 16].rearrange("p (k t) -> p k t", k=8)
    assert a_ks >= NFC * N
    gT = Rb[:, 0:NFC * N].rearrange("p (f t) -> p f t", f=NFC)
    NSF, NSB = 6, 6
    sf_t = sb("scrf", [128, NSF, 528], F32)
    sbf_t = sb("scrb", [128, NSB, 512], BF16)
    SF = Rot("sf", [sf_t[:, i, :] for i in range(NSF)])
    SBF = Rot("sb", [sbf_t[:, i, :] for i in range(NSB)])
    cmps = nc.psum_tensor("ps", [128, 8, 512], F32)
    ps_t = cmps.__enter__()
    ctxs.append(cmps)
    PS = Rot("ps", [ps_t[:, i, :] for i in range(6)])
    PS_ST = ps_t[:, 6, :]
    PS_X = ps_t[:, 7, :]

    WS = Rot("w", [wsl[:, i, :] for i in range(NSLOT)])

    def W(l, name):
        nk, wc = WSPEC[name]
        rb = WS.get()
        n = nk * wc
        woff = 0 if SMALLW else WOFF[name]
        src = wp_d[l, woff:woff + 128 * n].rearrange("(p n) -> p n", p=128)
        kb.dma("pool", rb.n, rb.ap[:, 0:n], src, writes=[rb])
        return rb.ap[:, 0:n].rearrange("p (k c) -> p k c", k=nk), rb

    def mm(out, lhsT, rhs, start, stop, reads, wname):
        kb.op("pe", lambda e: e.matmul(out, lhsT=lhsT, rhs=rhs, start=start, stop=stop), reads=reads, writes=[wname])

    def act(out, in_, func, reads, writes, **kw):
        kb.op("act", lambda e: e.activation(out=out, in_=in_, func=func, **kw), reads=reads, writes=writes)

    def tt(out, in0, in1, op, reads, writes, en="dve"):
        kb.op(en, lambda e: e.tensor_tensor(out=out, in0=in0, in1=in1, op=op), reads=reads, writes=writes)

    def ts(out, in0, s1, s2, op0, op1, reads, writes, en="dve"):
        kb.op(en, lambda e: e.tensor_scalar(out=out, in0=in0, scalar1=s1, scalar2=s2, op0=op0, op1=op1), reads=reads, writes=writes)

    def stt(out, in0, scalar, in1, op0, op1, reads, writes, en="dve"):
        kb.op(en, lambda e: e.scalar_tensor_tensor(out=out, in0=in0, scalar=scalar, in1=in1, op0=op0, op1=op1), reads=reads, writes=writes)

    def cp(l, off, n=1):
        return cpk[:, l * CP + off:l * CP + off + n]

    def rstd_from(ps_ap, ps_name, ncols, inv_n):
        lnv = SF.get()
        act(lnv.ap[:, 0:ncols], ps_ap, AF.Ln, [ps_name], [lnv.nm], scale=inv_n, bias=EPS)
        r = SF.get()
        act(r.ap[:, 0:ncols], lnv.ap[:, 0:ncols], AF.Exp, [lnv.nm], [r.nm], scale=-0.5)
        return r

    def norm_tile(l, n, goff):
        cols = slice(n * N, (n + 1) * N)
        for kc in range(8):
            sq = SBF.get()
            act(sq.ap, xT[:, kc, cols], AF.Square, ["xT"], [sq.nm])
            mm(PS_ST, ones_bf[:, :], sq.ap, kc == 0, kc == 7, [sq.nm, "const"], "ps_st")
        r = rstd_from(PS_ST, "ps_st", N, 1.0 / D)
        for kc in range(8):
            stt(hT[:, kc, :], xT[:, kc, cols], cp(l, goff + kc), r.ap[:, 0:N], ALU.mult, ALU.mult, ["xT", r.nm, "const"], ["hT"])

    def post_norm_residual(l, n, goff):
        cols = slice(n * N, (n + 1) * N)
        r = rstd_from(PS_ST, "ps_st", N, 1.0 / D)
        for m in range(8):
            stt(zbuf[:, m, :], zbuf[:, m, :], cp(l, goff + m), r.ap[:, 0:N], ALU.mult, ALU.mult, ["zbuf", r.nm, "const"], ["zbuf"])
            tt(xT[:, m, cols], xT[:, m, cols], zbuf[:, m, :], ALU.add, ["xT", "zbuf"], ["xT"])

    def gla_L(l, glT, tb):
        pl = PS.get()
        mm(pl.ap[:, 0:256], glT[0:17, tb], wgb[0:17, l * 256:(l + 1) * 256], True, True, ["glT", "const"], pl.nm)
        e1 = SF.get()
        act(e1.ap[:, 0:256], pl.ap[:, 0:256], AF.Exp, [pl.nm], [e1.nm], scale=-1.0)
        Lt = SF.get()
        act(Lt.ap[:, 0:256], e1.ap[:, 0:256], AF.Ln, [e1.nm], [Lt.nm], bias=1.0)
        return Lt

    def gla_glow(l, Wgk, wn):
        pg = PS.get()
        for kc in range(8):
            mm(pg.ap[0:16, :], Wgk[:, kc, 256:272], hT[:, kc, :], kc == 0, kc == 7, [wn, "hT"], pg.nm)
        kb.op("dve", lambda e: e.memset(glTt[:], 1.0), reads=["glT"], writes=["glT"])
        act(glTt[0:16, :], pg.ap[0:16, :], AF.Identity, [pg.nm, "glT"], ["glT"])
        return glTt

    def v_tok(Wv0, wn0, Wv1, wn1, tb, out_ap, out_name):
        pv = PS.get()
        for half, (Wv, wn) in enumerate(((Wv0, wn0), (Wv1, wn1))):
            for kc in range(8):
                mm(pv.ap[:, half * 256:(half + 1) * 256], hT[:, kc, tb], Wv[:, kc, :], kc == 0, kc == 7, [wn, "hT"], pv.nm)
        act(out_ap, pv.ap, AF.Identity, [pv.nm], [out_name])

    kb.dma("sp", "ld", xT[:], xT_d, writes=["xT"])
    kb.dma("sp", "ld", cpk[:], cp_d, writes=["const"])
    kb.dma("sp", "ld", wgb[:], wgb_d, writes=["const"])
    kb.dma("sp", "ld", cst[:], cst_d, writes=["const"])
    kb.dma("sp", "ld", pcs[:], pc_d, writes=["const"])
    TriA, TriB, cmask = cst[:, 0:128], cst[:, 128:256], cst[:, 256:384]
    kb.op("dve", lambda e: e.memset(ones_bf[:], 1.0), writes=["const"])
    kb.op("dve", lambda e: e.memset(onespad[:], 0.0), writes=["const"])
    kb.op("dve", lambda e: e.memset(onespad[:, 0, 0:64], 1.0), reads=["const"], writes=["const"])
    kb.op("dve", lambda e: e.memset(onespad[:, 1, 64:128], 1.0), reads=["const"], writes=["const"])
    kb.op("dve", lambda e: e.memset(Rb[:], 0.0), writes=["Vpad", "ksT", "gT"])
    for j in range(4):
        kb.op("dve", lambda e, j=j: e.tensor_copy(out=maskc[:, j, :], in_=cmask), reads=["const"], writes=["const"])
        ts(maskp[:, j, :], cmask, -1.0, 1.0, ALU.mult, ALU.add, ["const"], ["const"])
    for j in range(4):
        ts(maskp0[:, j, :], maskp[:, j, :], pcs[:, 16:17], None, ALU.mult, ALU.bypass, ["const"], ["const"])
    for l in range(depth):
        act(esink[:, l * 4:(l + 1) * 4], cp(l, 216, 4), AF.Exp, ["const"], ["const"])

    kb.barrier()
    CUT = int(os.environ.get("KDBG_CUT", "0"))
    STOPL = int(os.environ.get("KDBG_STOPL", "0"))

    def cut(k):
        if CUT == k:
            raise StopBuild()

    def body():
      for l in range(depth):
          if stop_after <= 0 and l == STOPL:
              break
          if phase != "C":
              kb.op("dve", lambda e: e.memset(S_run[:], 0.0), reads=["S_run"], writes=["S_run"])
              kb.op("dve", lambda e: e.memset(Ploc[:, 0, :], 1.0), reads=["Ploc"], writes=["Ploc"])
              for n in range(NT):
                  norm_tile(l, n, 0)
                  Wgk, wn_gk = W(l, "gkl")
                  glT = gla_glow(l, Wgk, wn_gk)
                  Wv0, wn0 = W(l, "gv0")
                  Wv1, wn1 = W(l, "gv1")
                  for blk in range(4):
                      b = n * 4 + blk
                      tb = slice(blk * 128, (blk + 1) * 128)
                      Lt = gla_L(l, glT, tb)
                      pk = PS.get()
                      for kc in range(8):
                          mm(pk.ap[:, 0:256], hT[:, kc, tb], Wgk[:, kc, 0:256], kc == 0, kc == 7, [wn_gk, "hT"], pk.nm)
                      prb = PS.get()
                      mm(prb.ap[:, 0:256], TriB, Lt.ap[:, 0:256], True, True, [Lt.nm, "const"], prb.nm)
                      erb = SF.get()
                      act(erb.ap[:, 0:256], prb.ap[:, 0:256], AF.Exp, [prb.nm], [erb.nm])
                      khat = SBF.get()
                      tt(khat.ap[:, 0:256], pk.ap[:, 0:256], erb.ap[:, 0:256], ALU.mult, [pk.nm, erb.nm], [khat.nm])
                      for p in range(2):
                          mm(PS_X[:, p * 2:p * 2 + 2], Lt.ap[:, p * 128:(p + 1) * 128], TriA[:, 126:128], True, True, [Lt.nm, "const"], "ps_x")
                      act(Ebt[:].rearrange("p q c -> p (q c)"), PS_X[:, 0:4], AF.Exp, ["ps_x"], ["Ebt"])
                      Vt = SBF.get()
                      v_tok(Wv0, wn0, Wv1, wn1, tb, Vt.ap, Vt.nm)
                      pd = PS.get()
                      for p in range(2):
                          mm(pd.ap[:, p * 256:(p + 1) * 256], khat.ap[:, p * 128:(p + 1) * 128], Vt.ap[:, p * 256:(p + 1) * 256], True, True, [khat.nm, Vt.nm], pd.nm)
                      kb.op("dve", lambda e, b=b: e.tensor_copy(out=Sloc[:, b, :, :], in_=S_run[:]), reads=["S_run"], writes=["Sloc"])
                      for p in range(2):
                          for hp in range(2):
                              pr = slice(hp * 64, (hp + 1) * 64)
                              stt(S_run[pr, p, :], S_run[pr, p, :], Ebt[pr, p, 1:2], pd.ap[pr, p * 256 + hp * 128:p * 256 + (hp + 1) * 128],
                                  ALU.mult, ALU.add, ["S_run", "Ebt", pd.nm], ["S_run"])
                      tt(Ploc[:, b + 1, :], Ploc[:, b, :], Ebt[:, :, 1], ALU.mult, ["Ploc", "Ebt"], ["Ploc"])
                  if n == NT - 1:
                      Wskv, wn_s = W(l, "skv")
                      tl = slice(384, 512)
                      pkt = PS.get()
                      for kc in range(8):
                          mm(pkt.ap[:, 0:128], Wskv[:, kc, 0:128], hT[:, kc, tl], kc == 0, kc == 7, [wn_s, "hT"], pkt.nm)
                      tsc = SF.get()
                      act(tsc.ap[:, 0:128], pkt.ap[:, 0:128], AF.Identity, [pkt.nm], [tsc.nm])
                      pvt = PS.get()
                      for kc in range(8):
                          mm(pvt.ap[:, 0:128], hT[:, kc, tl], Wskv[:, kc, 128:256], kc == 0, kc == 7, [wn_s, "hT"], pvt.nm)
                      act(tsc.ap[:, 128:256], pvt.ap[:, 0:128], AF.Identity, [pvt.nm, tsc.nm], [tsc.nm])
                      put = PS.get()
                      for gg in range(2):
                          Wpu, wn_p = W(l, "pu%d" % gg)
                          for g2 in range(2):
                              g = gg * 2 + g2
                              for kc in range(8):
                                  mm(put.ap[:, g * 16:(g + 1) * 16], Wpu[:, kc, g2 * 128:(g2 + 1) * 128], hT[:, kc, 496:512], kc == 0, kc == 7, [wn_p, "hT"], put.nm)
                      act(tsc.ap[:, 256:320], put.ap[:, 0:64], AF.Identity, [put.nm, tsc.nm], [tsc.nm])
                      kb.dma("sp", "cc", ccA_in[l][:, 264:584], tsc.ap[:, 0:320], reads=[tsc.nm], writes=["ccAi"])
              if stop_after <= 1 and l == STOPL:
                  break
              kb.dma("sp", "cc", ccA_in[l][:, 0:256], S_run[:].rearrange("p q v -> p (q v)"), reads=["S_run"], writes=["ccAi"])
              kb.op("dve", lambda e: e.memset(At8[:], 0.0), reads=["At8"], writes=["At8"])
              kb.op("dve", lambda e: e.tensor_copy(out=At8[:, 0:2], in_=Ploc[:, NB, :]), reads=["Ploc", "At8"], writes=["At8"])
              kb.dma("sp", "cc", ccA_in[l][:, 256:264], At8[:], reads=["At8"], writes=["ccAi"])
              if phase == "A":
                  break
              if phase is None:
                  kb.wait_all("pool", ["ccAi"])
                  kb.wait_all("pool", ["ccAo"])
                  kb.barrier_dma("pool")
                  ins = nc.gpsimd.collective_compute("AllGather", ALU.bypass, replica_groups=[list(range(NCORES))],
                                                     ins=[ccA_in[l].ap().opt()], outs=[ccA_out[l].ap().opt()])
                  sem = kb.dma_sem("ccx")
                  kb.dcnt["ccx"] += 1
                  ins.then_inc(sem, 1)
                  kb._record((sem, kb.dcnt["ccx"], "dma", True), [], ["ccAo"])
                  kb.wait_all("pool", ["ccAo"])
              gsrc = ccA_out[l].ap().rearrange("(r p) f -> p r f", p=128)
              zflat = zbuf[:].rearrange("p m t -> p (m t)")
              gFA = zflat[:, 0:8 * 264].rearrange("p (r f) -> p r f", r=8)
              kb.dma("sp", "ld", gFA, gsrc[:, :, 0:264], reads=["ccAo"], writes=["zbuf"])
              kb.op("dve", lambda e: e.memset(S_in[:], 0.0), reads=["S_in"], writes=["S_in"])
              for c2 in range(NCORES):
                  mcol = pcs[:, c2:c2 + 1]
                  Fm = SF.get()
                  ts(Fm.ap[:, 0:256], gFA[:, c2, 0:256], mcol, None, ALU.mult, ALU.bypass, ["zbuf", "const"], [Fm.nm])
                  ts(Am[:], gFA[:, c2, 256:258], -1.0, mcol, ALU.add, ALU.mult, ["zbuf", "const"], ["Am"])
                  ts(Am[:], Am[:], 1.0, None, ALU.add, ALU.bypass, ["Am"], ["Am"])
                  for p in range(2):
                      stt(S_in[:, p, :], S_in[:, p, :], Am[:, p:p + 1], Fm.ap[:, p * 128:(p + 1) * 128], ALU.mult, ALU.add, ["S_in", "Am", Fm.nm], ["S_in"])
              for b in range(NB):
                  for p in range(2):
                      stt(Sloc[:, b, p, :], S_in[:, p, :], Ploc[:, b, p:p + 1], Sloc[:, b, p, :], ALU.mult, ALU.add, ["S_in", "Ploc", "Sloc"], ["Sloc"])
              gTL = zflat[:, 0:8 * 320].rearrange("p (r f) -> p r f", r=8)
              kb.dma("sp", "ld", gTL, gsrc[:, :, 264:584], reads=["ccAo"], writes=["zbuf"])
              ts(halo[:], gTL[:, 0, :], pcs[:, 8:9], None, ALU.mult, ALU.bypass, ["zbuf", "const"], ["halo"])
              for c2 in range(1, NCORES):
                  stt(halo[:], gTL[:, c2, :], pcs[:, 8 + c2:9 + c2], halo[:], ALU.mult, ALU.add, ["zbuf", "const", "halo"], ["halo"])
              for g in range(2):
                  kb.op("dve", lambda e, g=g: e.tensor_copy(out=ksp[g * 64:(g + 1) * 64, g, 0:128], in_=halo[g * 64:(g + 1) * 64, 0:128]), reads=["halo"], writes=["ksT"])
              for g in range(2):
                  kb.op("dve", lambda e, g=g: e.tensor_copy(out=Vpad[:, 0, g, g * 64:(g + 1) * 64], in_=halo[:, 128 + g * 64:128 + (g + 1) * 64]), reads=["halo"], writes=["Vpad"])
              kb.op("dve", lambda e: e.tensor_copy(out=uh[:].rearrange("p g t -> p (g t)"), in_=halo[:, 256:320]), reads=["halo"], writes=["uh"])

              if stop_after <= 2 and l == STOPL:
                  break
              for n in range(NT):
                  cols = slice(n * N, (n + 1) * N)
                  norm_tile(l, n, 0)
                  Wgk, wn_gk = W(l, "gkl")
                  glT = gla_glow(l, Wgk, wn_gk)
                  Wv0, wn0 = W(l, "gv0")
                  Wv1, wn1 = W(l, "gv1")
                  for blk in range(4):
                      tb = slice(blk * 128, (blk + 1) * 128)
                      Lt = gla_L(l, glT, tb)
                      pb = PS.get()
                      for p in range(2):
                          mm(pb.ap[:, p * 128:(p + 1) * 128], Lt.ap[:, p * 128:(p + 1) * 128], TriA, True, True, [Lt.nm, "const"], pb.nm)
                      act(ebT[:, :, tb], pb.ap[:, 0:256].rearrange("p (q t) -> p q t", q=2), AF.Exp, [pb.nm], ["ebT"])
                      act(enbT[:, :, tb], pb.ap[:, 0:256].rearrange("p (q t) -> p q t", q=2), AF.Exp, [pb.nm], ["enbT"], scale=-1.0)
                      v_tok(Wv0, wn0, Wv1, wn1, tb, Vg[:, blk, :], "Vg")
                  for p in range(2):
                      pk = PS.get()
                      for kc in range(8):
                          mm(pk.ap, Wgk[:, kc, p * 128:(p + 1) * 128], hT[:, kc, :], kc == 0, kc == 7, [wn_gk, "hT"], pk.nm)
                      tt(kgT[:, p, :], pk.ap, enbT[:, p, :], ALU.mult, [pk.nm, "enbT"], ["kgT"])
                  Wq, wn_q = W(l, "gq")
                  for p in range(2):
                      pq = PS.get()
                      for kc in range(8):
                          mm(pq.ap, Wq[:, kc, p * 128:(p + 1) * 128], hT[:, kc, :], kc == 0, kc == 7, [wn_q, "hT"], pq.nm)
                      for hp in range(2):
                          pr = slice(hp * 64, (hp + 1) * 64)
                          stt(qgp[pr, 2 * p + hp, :], pq.ap[pr, :], 0.125, ebT[pr, p, :], ALU.mult, ALU.mult, [pq.nm, "ebT"], ["qgT"])
                  for hh in range(2):
                      Wr, wn_r = W(l, "gr%d" % hh)
                      for h2 in range(2):
                          h = hh * 2 + h2
                          pr_ = PS.get()
                          for kc in range(8):
                              mm(pr_.ap, Wr[:, kc, h2 * 128:(h2 + 1) * 128], hT[:, kc, :], kc == 0, kc == 7, [wn_r, "hT"], pr_.nm)
                          act(srT[:, h, :], pr_.ap, AF.Silu, [pr_.nm], ["srT"])
                  cut(1)
                  for blk in range(4):
                      b = n * 4 + blk
                      tb = slice(blk * 128, (blk + 1) * 128)
                      pa = PS.get()
                      for h in range(4):
                          p, hp = divmod(h, 2)
                          pr = slice(hp * 64, (hp + 1) * 64)
                          mm(pa.ap[:, h * 128:(h + 1) * 128], kgT[:, p, tb], qgp[:, h, tb], True, True, ["kgT", "qgT"], pa.nm)
                      at = SBF.get()
                      tt(at.ap, pa.ap, maskc[:].rearrange("p j q -> p (j q)"), ALU.mult, [pa.nm, "const"], [at.nm])
                      po = PS.get()
                      for h in range(4):
                          p, hp = divmod(h, 2)
                          pr = slice(hp * 64, (hp + 1) * 64)
                          mm(po.ap[:, h * 128:(h + 1) * 128], Vg[:, blk, h * 128:(h + 1) * 128], at.ap[:, h * 128:(h + 1) * 128], True, False, ["Vg", at.nm], po.nm)
                          mm(po.ap[:, h * 128:(h + 1) * 128], Sloc[:, b, p, :], qgp[:, h, tb], False, True, ["Sloc", "qgT"], po.nm)
                      sq = SBF.get()
                      act(sq.ap, po.ap, AF.Square, [po.nm], [sq.nm])
                      mm(PS_ST, ones_bf[:, :], sq.ap, True, True, [sq.nm, "const"], "ps_st")
                      r = rstd_from(PS_ST, "ps_st", N, 1.0 / 128.0)
                      t1 = SF.get()
                      tt(t1.ap[:, 0:N].rearrange("p (h t) -> p h t", h=4), r.ap[:, 0:N].rearrange("p (h t) -> p h t", h=4), srT[:, :, tb], ALU.mult, [r.nm, "srT"], [t1.nm])
                      for h in range(4):
                          stt(glao[:, h, tb], po.ap[:, h * 128:(h + 1) * 128], cp(l, 32 + h), t1.ap[:, h * 128:(h + 1) * 128], ALU.mult, ALU.mult, [po.nm, t1.nm, "const"], ["glao"])
                  cut(2)
                  for jj in range(2):
                      Wsq, wn_sq = W(l, "sq%d" % jj)
                      for j2 in range(2):
                          j = jj * 2 + j2
                          pq = PS.get()
                          for kc in range(8):
                              mm(pq.ap, Wsq[:, kc, j2 * 128:(j2 + 1) * 128], hT[:, kc, :], kc == 0, kc == 7, [wn_sq, "hT"], pq.nm)
                          act(qsT[:, j, :], pq.ap, AF.Identity, [pq.nm], ["qsT"])
                  Wskv, wn_s = W(l, "skv")
                  pk = PS.get()
                  for kc in range(8):
                      mm(pk.ap, Wskv[:, kc, 0:128], hT[:, kc, :], kc == 0, kc == 7, [wn_s, "hT"], pk.nm)
                  for g in range(2):
                      act(ksp[g * 64:(g + 1) * 64, g, 128:640], pk.ap[g * 64:(g + 1) * 64, :], AF.Identity, [pk.nm], ["ksT"])
                  for blk in range(4):
                      tb = slice(blk * 128, (blk + 1) * 128)
                      pv = PS.get()
                      for kc in range(8):
                          mm(pv.ap[:, 0:128], hT[:, kc, tb], Wskv[:, kc, 128:256], kc == 0, kc == 7, [wn_s, "hT"], pv.nm)
                      for g in range(2):
                          act(Vpad[:, 1 + blk, g, g * 64:(g + 1) * 64], pv.ap[:, g * 64:(g + 1) * 64], AF.Identity, [pv.nm], ["Vpad"])
                  for blk in range(4):
                      b = n * 4 + blk
                      tb = slice(blk * 128, (blk + 1) * 128)
                      probs = []
                      for g in range(2):
                          pr = slice(g * 64, (g + 1) * 64)
                          for which in range(2):
                              kcols = slice(128 + blk * 128, 256 + blk * 128) if which == 0 else slice(blk * 128, 128 + blk * 128)
                              psc = PS.get()
                              mm(psc.ap.rearrange("p (j q) -> p j q", j=4), ksp[:, g, kcols], qsT[:, :, tb], True, True, ["ksT", "qsT"], psc.nm)
                              pe_ = SBF.get()
                              act(pe_.ap, psc.ap, AF.Exp, [psc.nm], [pe_.nm], scale=0.125)
                              msk = maskc if which == 0 else (maskp0 if b == 0 else maskp)
                              tt(pe_.ap, pe_.ap, msk[:].rearrange("p j q -> p (j q)"), ALU.mult, [pe_.nm, "const"], [pe_.nm])
                              probs.append((pe_, g, which))
                      po = PS.get()
                      pdn = PS.get()
                      for i, (pe_, g, which) in enumerate(probs):
                          vb = 1 + blk if which == 0 else blk
                          mm(po.ap, Vpad[:, vb, g, :], pe_.ap, i == 0, i == 3, ["Vpad", pe_.nm], po.nm)
                      for i, (pe_, g, which) in enumerate(probs):
                          mm(pdn.ap, onespad[:, g, :], pe_.ap, i == 0, i == 3, ["const", pe_.nm], pdn.nm)
                      den = SF.get()
                      for j in range(4):
                          ts(den.ap[:, j * 128:(j + 1) * 128], pdn.ap[:, j * 128:(j + 1) * 128], esink[:, l * 4 + j:l * 4 + j + 1], None, ALU.add, ALU.bypass, [pdn.nm, "const", den.nm], [den.nm])
                      kb.op("dve", lambda e, den=den: e.reciprocal(out=den.ap[:, 0:N], in_=den.ap[:, 0:N]), reads=[den.nm], writes=[den.nm])
                      tt(swao[:, :, tb], po.ap.rearrange("p (j q) -> p j q", j=4), den.ap[:, 0:N].rearrange("p (j q) -> p j q", j=4), ALU.mult, [po.nm, den.nm], ["swao"])
                  for g in range(2):
                      kb.op("dve", lambda e, g=g: e.tensor_copy(out=ksp[g * 64:(g + 1) * 64, g, 0:128], in_=ksp[g * 64:(g + 1) * 64, g, 512:640]), reads=["ksT"], writes=["ksT"])
                  for g in range(2):
                      kb.op("dve", lambda e, g=g: e.tensor_copy(out=Vpad[:, 0, g, g * 64:(g + 1) * 64], in_=Vpad[:, 4, g, g * 64:(g + 1) * 64]), reads=["Vpad"], writes=["Vpad"])
                  cut(3)
                  Wpw, wn_pw = W(l, "pw")
                  for gg in range(2):
                      Wpu, wn_p = W(l, "pu%d" % gg)
                      for g2 in range(2):
                          g = gg * 2 + g2
                          w = 2 ** (g + 1)
                          pu_ = PS.get()
                          for kc in range(8):
                              mm(pu_.ap, Wpu[:, kc, g2 * 128:(g2 + 1) * 128], hT[:, kc, :], kc == 0, kc == 7, [wn_p, "hT"], pu_.nm)
                          ub = SF.get()
                          act(ub.ap[:, 16:528], pu_.ap, AF.Identity, [pu_.nm], [ub.nm])
                          kb.op("dve", lambda e, ub=ub, g=g: e.tensor_copy(out=ub.ap[:, 0:16], in_=uh[:, g, :]), reads=["uh", ub.nm], writes=[ub.nm])
                          kb.op("dve", lambda e, ub=ub, g=g: e.tensor_copy(out=uh[:, g, :], in_=ub.ap[:, 512:528]), reads=[ub.nm, "uh"], writes=["uh"])
                          cur = ub
                          lo, sh = 0, 1
                          while sh < w:
                              nx = SF.get()
                              lo2 = lo + sh
                              tt(nx.ap[:, lo2:528], cur.ap[:, lo2:528], cur.ap[:, lo2 - sh:528 - sh], ALU.add, [cur.nm], [nx.nm])
                              cur, lo, sh = nx, lo2, sh * 2
                          dT = SBF.get()
                          stt(dT.ap, cur.ap[:, 16:528], 1.0 / w, ub.ap[:, 16:528], ALU.mult, ALU.subtract, [cur.nm, ub.nm], [dT.nm])
                          if n == 0:
                              tmp = SF.get()
                              tt(tmp.ap[:, 0:16], cur.ap[:, 16:32], pcs[:, 32 + g * 16:48 + g * 16], ALU.mult, [cur.nm, "const"], [tmp.nm])
                              tt(dT.ap[:, 0:16], tmp.ap[:, 0:16], ub.ap[:, 16:32], ALU.subtract, [tmp.nm, ub.nm, dT.nm], [dT.nm])
                          pp = PS.get()
                          mm(pp.ap, Wpw[:, g, :], dT.ap, True, True, [wn_pw, dT.nm], pp.nm)
                          ts(pcT[:, g, :], pp.ap, cp(l, 36 + g), None, ALU.mult, ALU.bypass, [pp.nm, "const"], ["pcT"])
                  cut(4)
                  for m in range(8):
                      Wgt, wn_g = W(l, "gt%d" % m)
                      Wbr, wn_b = W(l, "br%d" % m)
                      sigs = []
                      for bi in range(3):
                          pg = PS.get()
                          for kc in range(8):
                              mm(pg.ap, Wgt[:, kc, bi * 128:(bi + 1) * 128], hT[:, kc, :], kc == 0, kc == 7, [wn_g, "hT"], pg.nm)
                          sg = SBF.get()
                          act(sg.ap, pg.ap, AF.Sigmoid, [pg.nm], [sg.nm])
                          sigs.append(sg)
                      t0 = None
                      for bi, (src, sname) in enumerate(((glao, "glao"), (swao, "swao"), (pcT, "pcT"))):
                          py = PS.get()
                          for c in range(4):
                              mm(py.ap, Wbr[:, bi * 4 + c, :], src[:, c, :], c == 0, c == 3, [wn_b, sname], py.nm)
                          if bi == 0:
                              t0 = SF.get()
                              tt(t0.ap[:, 0:N], py.ap, sigs[0].ap, ALU.mult, [py.nm, sigs[0].nm], [t0.nm])
                          else:
                              t1 = SF.get()
                              tt(t1.ap[:, 0:N], py.ap, sigs[bi].ap, ALU.mult, [py.nm, sigs[bi].nm], [t1.nm])
                              if bi == 1:
                                  tt(t0.ap[:, 0:N], t0.ap[:, 0:N], t1.ap[:, 0:N], ALU.add, [t0.nm, t1.nm], [t0.nm])
                              else:
                                  tt(mrg[:, m, :], t0.ap[:, 0:N], t1.ap[:, 0:N], ALU.add, [t0.nm, t1.nm], ["mrg"])
                  cut(5)
                  for m2 in range(8):
                      Wo, wn_o = W(l, "wo%d" % m2)
                      pz = PS.get()
                      for m in range(8):
                          mm(pz.ap, Wo[:, m, :], mrg[:, m, :], m == 0, m == 7, [wn_o, "mrg"], pz.nm)
                      act(zbuf[:, m2, :], pz.ap, AF.Identity, [pz.nm], ["zbuf"])
                      sq = SBF.get()
                      act(sq.ap, pz.ap, AF.Square, [pz.nm], [sq.nm])
                      mm(PS_ST, ones_bf[:, :], sq.ap, m2 == 0, m2 == 7, [sq.nm, "const"], "ps_st")
                  post_norm_residual(l, n, 8)

          if stop_after <= 3 and l == STOPL:
              break
          if phase == "B":
              break
          kb.barrier()
          if phase is None:
              kb.op("dve", lambda e: e.tensor_copy(out=xt2[:], in_=xT[:, :, TC - 2:TC]), reads=["xT", "xt2"], writes=["xt2"])
              kb.dma("sp", "cc", ccB_in[l][:, :], xt2[:].rearrange("p k t -> p (k t)"), reads=["xt2"], writes=["ccBi"])
              kb.wait_all("pool", ["ccBi"])
              kb.wait_all("pool", ["ccBo"])
              kb.barrier_dma("pool")
              ins = nc.gpsimd.collective_compute("AllGather", ALU.bypass, replica_groups=[list(range(NCORES))],
                                                 ins=[ccB_in[l].ap().opt()], outs=[ccB_out[l].ap().opt()])
              sem = kb.dma_sem("ccx")
              kb.dcnt["ccx"] += 1
              ins.then_inc(sem, 1)
              kb._record((sem, kb.dcnt["ccx"], "dma", True), [], ["ccBo"])
              kb.wait_all("pool", ["ccBo"])
          gB = zbuf[:].rearrange("p m t -> p (m t)")[:, 0:8 * XB].rearrange("p (r f) -> p r f", r=8)
          kb.dma("sp", "ld", gB, ccB_out[l].ap().rearrange("(r p) f -> p r f", p=128), reads=["ccBo"], writes=["zbuf"])
          ts(xh[:], gB[:, 0, :], pcs[:, 8:9], None, ALU.mult, ALU.bypass, ["zbuf", "const"], ["xh"])
          for c2 in range(1, NCORES):
              stt(xh[:], gB[:, c2, :], pcs[:, 8 + c2:9 + c2], xh[:], ALU.mult, ALU.add, ["zbuf", "const", "xh"], ["xh"])
          xh3 = xh[:].rearrange("p (k t) -> p k t", k=8)
          sqh = SBF.get()
          act(sqh.ap[:, 0:16], xh[:], AF.Square, ["xh"], [sqh.nm])
          for kc in range(8):
              mm(PS_X[:, 8:10], ones_bf[:, :], sqh.ap[:, kc * 2:kc * 2 + 2], kc == 0, kc == 7, [sqh.nm, "const"], "ps_x")
          rh = rstd_from(PS_X[:, 8:10], "ps_x", 2, 1.0 / D)
          for kc in range(8):
              stt(h2h[:, kc, :], xh3[:, kc, :], cp(l, 16 + kc), rh.ap[:, 0:2], ALU.mult, ALU.mult, ["xh", rh.nm, "const"], ["h2h"])

          if stop_after <= 4 and l == STOPL:
              break
          for n in range(NT):
              norm_tile(l, n, 16)
              for j in range(NFC):
                  Wu, wn_u = W(l, "up%d" % j)
                  gact = None
                  for half in range(2):
                      cidx = j + NFC * half
                      pu_ = PS.get()
                      for kc in range(8):
                          mm(pu_.ap, Wu[:, kc, half * 128:(half + 1) * 128], hT[:, kc, :], kc == 0, kc == 7, [wn_u, "hT"], pu_.nm)
                      U = SF.get()
                      act(U.ap[:, 2:514], pu_.ap, AF.Identity, [pu_.nm], [U.nm])
                      if n == 0:
                          for kc in range(8):
                              mm(PS_X[:, 16:18], Wu[:, kc, half * 128:(half + 1) * 128], h2h[:, kc, :], kc == 0, kc == 7, [wn_u, "h2h"], "ps_x")
                          act(U.ap[:, 0:2], PS_X[:, 16:18], AF.Identity, ["ps_x", U.nm], [U.nm])
                      else:
                          kb.op("dve", lambda e, U=U, cidx=cidx: e.tensor_copy(out=U.ap[:, 0:2], in_=uph[:, cidx, :]), reads=["uph", U.nm], writes=[U.nm])
                      kb.op("dve", lambda e, U=U, cidx=cidx: e.tensor_copy(out=uph[:, cidx, :], in_=U.ap[:, 512:514]), reads=[U.nm, "uph"], writes=["uph"])
                      acc = SF.get()
                      act(acc.ap[:, 0:N], pu_.ap, AF.Identity, [pu_.nm, "const"], [acc.nm], scale=cp(l, 40 + 2 * 44 + cidx), bias=cp(l, 172 + cidx))
                      stt(acc.ap[:, 0:N], U.ap[:, 1:513], cp(l, 40 + 1 * 44 + cidx), acc.ap[:, 0:N], ALU.mult, ALU.add, [U.nm, acc.nm, "const"], [acc.nm])
                      stt(acc.ap[:, 0:N], U.ap[:, 0:512], cp(l, 40 + cidx), acc.ap[:, 0:N], ALU.mult, ALU.add, [U.nm, acc.nm, "const"], [acc.nm])
                      if half == 0:
                          gact = SF.get()
                          act(gact.ap[:, 0:N], acc.ap[:, 0:N], AF.Gelu_apprx_tanh, [acc.nm], [gact.nm])
                      else:
                          tt(gT[:, j, :], gact.ap[:, 0:N], acc.ap[:, 0:N], ALU.mult, [gact.nm, acc.nm], ["gT"])
              for m in range(8):
                  Wd, wn_d = W(l, "dn%d" % m)
                  pz = PS.get()
                  for fc in range(NFC):
                      mm(pz.ap, Wd[:, fc, :], gT[:, fc, :], fc == 0, fc == NFC - 1, [wn_d, "gT"], pz.nm)
                  act(zbuf[:, m, :], pz.ap, AF.Identity, [pz.nm], ["zbuf"])
                  sq = SBF.get()
                  act(sq.ap, pz.ap, AF.Square, [pz.nm], [sq.nm])
                  mm(PS_ST, ones_bf[:, :], sq.ap, m == 0, m == 7, [sq.nm, "const"], "ps_st")
              post_norm_residual(l, n, 24)
          kb.barrier()

    try:
        body()
    except StopBuild:
        pass
    if phase == "A":
        kb.wait_all("sp", ["ccAi"])
    else:
        kb.dma("sp", "st", yT_d, xT[:], reads=["xT"], writes=["yT"])
        kb.wait_all("sp", ["yT"])
    kb.barrier()
    for cm in reversed(ctxs):
        cm.__exit__(None, None, None)
    kb.close()
    return nc, kb


_CACHE = {}
FUSED = True


def _launch(inp, depth, stop_after, xTs, phase=None, extra=None, outs=("yT",)):
    wpack = np.stack([_pack_weights(inp, l) for l in range(depth)], axis=0)
    if os.environ.get("KDBG_SMALLW"):
        wpack = np.ascontiguousarray(wpack[:, :128 * SLOT])
    cpack = _pack_small(inp)
    wgb = np.zeros((17, DEPTH * 256), np.float32)
    for l in range(DEPTH):
        wgb[0:16, l * 256:(l + 1) * 256] = inp["gla_w_gate_up"][l]
        wgb[16, l * 256:(l + 1) * 256] = inp["gla_b_gate"][l]
    in_maps = []
    for c in range(NCORES):
        cst, pc = _consts(c)
        pad = float(c) if phase is None else 0.0
        wpc = np.concatenate([wpack, np.full((1, wpack.shape[1]), pad, np.float32)], axis=0)
        m = {"xT": xTs[c], "wpack": wpc, "cpack": cpack, "wgb": wgb, "cst": cst, "pcst": pc}
        if extra:
            m.update(extra)
        in_maps.append(m)
    key = (depth, stop_after, phase)
    if key not in _CACHE:
        _CACHE[key] = build(depth, stop_after, phase)[0]
    res = run_bass_kernel_spmd(_CACHE[key], in_maps, core_ids=list(range(NCORES)))
    if os.environ.get("KDBG_VERBOSE"):
        print("launch done depth", depth, phase, flush=True)
    return [[np.asarray(res.results[c][o]) for c in range(NCORES)] for o in outs]


def kernel(**inp):
    inp = {k: np.asarray(v) for k, v in inp.items()}
    depth = int(inp.pop("_depth", DEPTH)) if "_depth" in inp else DEPTH
    stop_after = int(inp.pop("_stop", 99)) if "_stop" in inp else 99
    fused = bool(inp.pop("_fused", FUSED)) if "_fused" in inp else FUSED
    x = inp["x"][0]
    xTs = []
    for c in range(NCORES):
        xs = x[c * TC:(c + 1) * TC]
        xTs.append(np.ascontiguousarray(xs.T.reshape(8, 128, TC).transpose(1, 0, 2)))
    if fused:
        xTs = _launch(inp, depth, stop_after, xTs)[0]
    else:
        for l in range(depth):
            inp_l = {k: (v if k == "x" else np.repeat(v[l:l + 1], DEPTH, axis=0)) for k, v in inp.items()}
            payA = _launch(inp_l, 1, 99, xTs, phase="A", outs=("payA",))[0]
            gA = np.ascontiguousarray(np.concatenate(payA, axis=0))
            xTs = _launch(inp_l, 1, 99, xTs, phase="B", extra={"gA": gA})[0]
            gB = np.ascontiguousarray(np.concatenate([t[:, :, TC - 2:TC].reshape(128, XB) for t in xTs], axis=0))
            xTs = _launch(inp_l, 1, 99, xTs, phase="C", extra={"gB": gB})[0]
    out = np.empty((1, T_ALL, D), np.float32)
    for c in range(NCORES):
        out[0, c * TC:(c + 1) * TC] = xTs[c].transpose(1, 0, 2).reshape(D, TC).T
    return out
```
